# Optimizing a Trainium2 kernel written in Bass

```python
import math
import jax, jax.numpy as jnp
from jax import lax
import numpy as np

D_MODEL = 1024
BATCH = 4
SEQ = 4096
DEPTH = 2

GRID_W = 64
D_FF = 2816
NORM_EPS = 1e-6
N_BRANCHES = 3
S5_GROUPS = 32
S5_GROUP_CH = 16
S5_STATE = 64
S5_WIDTH = S5_GROUPS * S5_GROUP_CH
S5_DT_MIN = 1e-3
S5_DT_MAX = 1e-1
GLA_HEADS = 4
GLA_HEAD_DIM = 128
GLA_WIDTH = GLA_HEADS * GLA_HEAD_DIM
GLA_LOWRANK = 16
GLA_TAU = 16.0
GLA_CHUNK = 64
ATTN_Q_HEADS = 8
ATTN_KV_HEADS = 2
ATTN_HEAD_DIM = 64
ATTN_WIDTH = ATTN_Q_HEADS * ATTN_HEAD_DIM
ATTN_KV_WIDTH = ATTN_KV_HEADS * ATTN_HEAD_DIM
ATTN_BLOCK = 128
ROPE_BASE = 10000.0
IN_SPLITS = (S5_WIDTH, GLA_WIDTH, GLA_WIDTH, GLA_WIDTH, GLA_WIDTH, GLA_LOWRANK, GLA_LOWRANK, ATTN_WIDTH, ATTN_KV_WIDTH, ATTN_KV_WIDTH)
IN_WIDTH = sum(IN_SPLITS)

kernel_name = 'hybrid_s5_gla_gqa_macaron_encoder'

F32 = jnp.float32


def rms_norm(x, gain):
    x32 = x.astype(F32)
    y = x32 * lax.rsqrt(jnp.mean(x32 * x32, axis=-1, keepdims=True) + NORM_EPS)
    return (y * gain.astype(F32)).astype(x.dtype)


def swiglu_ffn(h, w_gate, w_up, w_down):
    return (jax.nn.silu(h @ w_gate) * (h @ w_up)) @ w_down


def _linear_recurrence(left, right):
    a_l, b_l = left
    a_r, b_r = right
    return a_r * a_l, a_r * b_l + b_r


def s5_scan_dir(u32, lam_re, lam_im, log_dt, b_re, b_im, c_re, c_im, reverse):
    lam = lax.complex(lam_re.astype(F32), lam_im.astype(F32))
    dt = jnp.exp(log_dt.astype(F32))[:, None]
    lam_bar = jnp.exp(lam * dt)
    b = lax.complex(b_re.astype(F32), b_im.astype(F32))
    b_bar = ((lam_bar - 1.0) / lam)[..., None] * b
    bu = jnp.einsum('blgh,gph->blgp', u32.astype(jnp.complex64), b_bar)
    a = jnp.broadcast_to(lam_bar, bu.shape)
    _, states = lax.associative_scan(_linear_recurrence, (a, bu), axis=1, reverse=reverse)
    c = lax.complex(c_re.astype(F32), c_im.astype(F32))
    return jnp.real(jnp.einsum('blgp,ghp->blgh', states, c))


def s5_branch(u, lam_re, lam_im, log_dt, b_re, b_im, c_re, c_im, d_skip, w_glu):
    bsz, seq_len, _ = u.shape
    u32 = u.astype(F32)
    ug = u32.reshape(bsz, seq_len, S5_GROUPS, S5_GROUP_CH)
    y = (s5_scan_dir(ug, lam_re[0], lam_im[0], log_dt[0], b_re[0], b_im[0], c_re[0], c_im[0], False)
         + s5_scan_dir(ug, lam_re[1], lam_im[1], log_dt[1], b_re[1], b_im[1], c_re[1], c_im[1], True))
    y = y.reshape(bsz, seq_len, S5_WIDTH) + d_skip.astype(F32) * u32
    y = jax.nn.gelu(y).astype(u.dtype)
    return y * jax.nn.sigmoid(y @ w_glu)


def gla_chunked(q, k, v, log_a):
    bsz, seq_len, nh, dk = q.shape
    dv = v.shape[-1]
    n_chunks = seq_len // GLA_CHUNK

    def chunks(t):
        return t.reshape(bsz, n_chunks, GLA_CHUNK, nh, t.shape[-1])

    q, k, v, log_a = chunks(q), chunks(k), chunks(v), chunks(log_a)
    b = jnp.cumsum(log_a, axis=2)
    b_last = b[:, :, -1]
    q_dec = q * jnp.exp(b)
    k_dec = k * jnp.exp(-b)
    mask = jnp.tril(jnp.ones((GLA_CHUNK, GLA_CHUNK), dtype=bool))
    scores = jnp.where(mask, jnp.einsum('bnihd,bnjhd->bnhij', q_dec, k_dec), 0.0)
    o_intra = jnp.einsum('bnhij,bnjhe->bnihe', scores, v)
    k_to_end = k * jnp.exp(b_last[:, :, None] - b)
    chunk_kv = jnp.einsum('bnjhd,bnjhe->nbhde', k_to_end, v)
    chunk_decay = jnp.exp(jnp.moveaxis(b_last, 1, 0))

    def step(state, inp):
        decay, kv = inp
        return decay[..., None] * state + kv, state

    init = jnp.zeros((bsz, nh, dk, dv), F32)
    _, prev_states = lax.scan(step, init, (chunk_decay, chunk_kv))
    o_inter = jnp.einsum('bnihd,nbhde->bnihe', q_dec, prev_states)
    return (o_intra + o_inter).reshape(bsz, seq_len, nh, dv)


def gla_branch(q, k, v, gate, z_f, z_b, w_alpha, b_alpha, norm_gain):
    bsz, seq_len, _ = q.shape

    def heads(t):
        return t.astype(F32).reshape(bsz, seq_len, GLA_HEADS, GLA_HEAD_DIM)

    qh = heads(q) * GLA_HEAD_DIM ** -0.5
    kh = heads(k)
    vh = heads(v)

    def log_gate(z, w, bias):
        logits = (z @ w + bias).astype(F32)
        return heads(jax.nn.log_sigmoid(logits) / GLA_TAU)

    la_f = log_gate(z_f, w_alpha[0], b_alpha[0])
    la_b = log_gate(z_b, w_alpha[1], b_alpha[1])

    def flip(t):
        return jnp.flip(t, axis=1)

    o_f = gla_chunked(qh, kh, vh, la_f)
    o_b = flip(gla_chunked(flip(qh), flip(kh), flip(vh), flip(la_b)))
    o = rms_norm(o_f + o_b, norm_gain).reshape(bsz, seq_len, GLA_WIDTH).astype(q.dtype)
    return o * jax.nn.silu(gate)


def rope_1d(x, pos):
    d = x.shape[-1]
    half = d // 2
    inv_freq = ROPE_BASE ** (-jnp.arange(half, dtype=F32) * 2.0 / d)
    ang = pos.astype(F32)[:, None] * inv_freq[None, :]
    cos = jnp.cos(ang)[:, None, :]
    sin = jnp.sin(ang)[:, None, :]
    x1, x2 = x[..., :half], x[..., half:]
    return jnp.concatenate([x1 * cos - x2 * sin, x2 * cos + x1 * sin], axis=-1)


def axial_rope(x, rows, cols):
    half = x.shape[-1] // 2
    return jnp.concatenate([rope_1d(x[..., :half], rows), rope_1d(x[..., half:], cols)], axis=-1)


def attn_branch(q, k, v, q_gain, k_gain):
    bsz, seq_len, _ = q.shape
    n_rows = seq_len // GRID_W
    rows = jnp.repeat(jnp.arange(n_rows, dtype=jnp.int32), GRID_W)
    cols = jnp.tile(jnp.arange(GRID_W, dtype=jnp.int32), n_rows)
    qh = q.astype(F32).reshape(bsz, seq_len, ATTN_Q_HEADS, ATTN_HEAD_DIM)
    kh = k.astype(F32).reshape(bsz, seq_len, ATTN_KV_HEADS, ATTN_HEAD_DIM)
    vh = v.astype(F32).reshape(bsz, seq_len, ATTN_KV_HEADS, ATTN_HEAD_DIM)
    qh = axial_rope(rms_norm(qh, q_gain), rows, cols) * ATTN_HEAD_DIM ** -0.5
    kh = axial_rope(rms_norm(kh, k_gain), rows, cols)
    group = ATTN_Q_HEADS // ATTN_KV_HEADS
    n_blocks = seq_len // ATTN_BLOCK
    q_blocks = qh.reshape(bsz, n_blocks, ATTN_BLOCK, ATTN_KV_HEADS, group, ATTN_HEAD_DIM)
    q_blocks = jnp.moveaxis(q_blocks, 1, 0)

    def attend(qb):
        s = jnp.einsum('bqkgd,bskd->bkgqs', qb, kh)
        p = jax.nn.softmax(s, axis=-1)
        return jnp.einsum('bkgqs,bskd->bqkgd', p, vh)

    out = lax.map(attend, q_blocks)
    out = jnp.moveaxis(out, 0, 1).reshape(bsz, seq_len, ATTN_WIDTH)
    return out.astype(q.dtype)


def setup_inputs(seed: int = 0) -> dict:
    key = jax.random.key(seed)
    keys = iter(jax.random.split(key, 40))

    def normal(shape, scale):
        return scale * jax.random.normal(next(keys), shape, F32)

    def gain(shape):
        return 1.0 + normal(shape, 0.02)

    G, H, P = S5_GROUPS, S5_GROUP_CH, S5_STATE
    x = normal((BATCH, SEQ, D_MODEL), 1.0)
    ffn1_norm = gain((DEPTH, D_MODEL))
    ffn1_w_gate = normal((DEPTH, D_MODEL, D_FF), D_MODEL ** -0.5)
    ffn1_w_up = normal((DEPTH, D_MODEL, D_FF), D_MODEL ** -0.5)
    ffn1_w_down = normal((DEPTH, D_FF, D_MODEL), D_FF ** -0.5)
    mix_norm = gain((DEPTH, D_MODEL))
    w_in = normal((DEPTH, D_MODEL, IN_WIDTH), D_MODEL ** -0.5)
    s5_lambda_re = -0.5 + normal((DEPTH, 2, G, P), 0.01)
    s5_lambda_im = math.pi * jnp.arange(P, dtype=F32) + normal((DEPTH, 2, G, P), 0.01)
    s5_log_dt = jax.random.uniform(next(keys), (DEPTH, 2, G), F32, math.log(S5_DT_MIN), math.log(S5_DT_MAX))
    s5_b_re = normal((DEPTH, 2, G, P, H), (0.5 / H) ** 0.5)
    s5_b_im = normal((DEPTH, 2, G, P, H), (0.5 / H) ** 0.5)
    s5_c_re = normal((DEPTH, 2, G, H, P), (0.5 / P) ** 0.5)
    s5_c_im = normal((DEPTH, 2, G, H, P), (0.5 / P) ** 0.5)
    s5_d = normal((DEPTH, S5_WIDTH), 1.0)
    s5_w_glu = normal((DEPTH, S5_WIDTH, S5_WIDTH), S5_WIDTH ** -0.5)
    gla_w_alpha = normal((DEPTH, 2, GLA_LOWRANK, GLA_WIDTH), GLA_LOWRANK ** -0.5)
    gla_b_alpha = normal((DEPTH, 2, GLA_WIDTH), 0.1)
    gla_norm = gain((DEPTH, GLA_HEAD_DIM))
    attn_q_norm = gain((DEPTH, ATTN_HEAD_DIM))
    attn_k_norm = gain((DEPTH, ATTN_HEAD_DIM))
    w_branch_s5 = normal((DEPTH, S5_WIDTH, D_MODEL), S5_WIDTH ** -0.5)
    w_branch_gla = normal((DEPTH, GLA_WIDTH, D_MODEL), GLA_WIDTH ** -0.5)
    w_branch_attn = normal((DEPTH, ATTN_WIDTH, D_MODEL), ATTN_WIDTH ** -0.5)
    w_merge_gate = normal((DEPTH, D_MODEL, N_BRANCHES * D_MODEL), D_MODEL ** -0.5)
    b_merge_gate = normal((DEPTH, N_BRANCHES * D_MODEL), 0.01)
    w_out = normal((DEPTH, D_MODEL, D_MODEL), D_MODEL ** -0.5)
    ffn2_norm = gain((DEPTH, D_MODEL))
    ffn2_w_gate = normal((DEPTH, D_MODEL, D_FF), D_MODEL ** -0.5)
    ffn2_w_up = normal((DEPTH, D_MODEL, D_FF), D_MODEL ** -0.5)
    ffn2_w_down = normal((DEPTH, D_FF, D_MODEL), D_FF ** -0.5)
    final_norm = gain((D_MODEL,))
    return {'x': x, 'ffn1_norm': ffn1_norm, 'ffn1_w_gate': ffn1_w_gate, 'ffn1_w_up': ffn1_w_up,
            'ffn1_w_down': ffn1_w_down, 'mix_norm': mix_norm, 'w_in': w_in,
            's5_lambda_re': s5_lambda_re, 's5_lambda_im': s5_lambda_im, 's5_log_dt': s5_log_dt,
            's5_b_re': s5_b_re, 's5_b_im': s5_b_im, 's5_c_re': s5_c_re, 's5_c_im': s5_c_im,
            's5_d': s5_d, 's5_w_glu': s5_w_glu, 'gla_w_alpha': gla_w_alpha, 'gla_b_alpha': gla_b_alpha,
            'gla_norm': gla_norm, 'attn_q_norm': attn_q_norm, 'attn_k_norm': attn_k_norm,
            'w_branch_s5': w_branch_s5, 'w_branch_gla': w_branch_gla, 'w_branch_attn': w_branch_attn,
            'w_merge_gate': w_merge_gate, 'b_merge_gate': b_merge_gate, 'w_out': w_out,
            'ffn2_norm': ffn2_norm, 'ffn2_w_gate': ffn2_w_gate, 'ffn2_w_up': ffn2_w_up,
            'ffn2_w_down': ffn2_w_down, 'final_norm': final_norm}


def reference(x, ffn1_norm, ffn1_w_gate, ffn1_w_up, ffn1_w_down, mix_norm, w_in,
              s5_lambda_re, s5_lambda_im, s5_log_dt, s5_b_re, s5_b_im, s5_c_re, s5_c_im,
              s5_d, s5_w_glu, gla_w_alpha, gla_b_alpha, gla_norm, attn_q_norm, attn_k_norm,
              w_branch_s5, w_branch_gla, w_branch_attn, w_merge_gate, b_merge_gate, w_out,
              ffn2_norm, ffn2_w_gate, ffn2_w_up, ffn2_w_down, final_norm):
    bsz, seq_len, _ = x.shape
    split_at = [int(c) for c in np.cumsum(IN_SPLITS)[:-1]]
    for i in range(DEPTH):
        h = rms_norm(x, ffn1_norm[i])
        x = x + 0.5 * swiglu_ffn(h, ffn1_w_gate[i], ffn1_w_up[i], ffn1_w_down[i])

        h = rms_norm(x, mix_norm[i])
        (s5_u, gla_q, gla_k, gla_v, gla_g, gla_zf, gla_zb,
         at_q, at_k, at_v) = jnp.split(h @ w_in[i], split_at, axis=-1)
        y_s5 = s5_branch(s5_u, s5_lambda_re[i], s5_lambda_im[i], s5_log_dt[i], s5_b_re[i], s5_b_im[i],
                         s5_c_re[i], s5_c_im[i], s5_d[i], s5_w_glu[i])
        y_gla = gla_branch(gla_q, gla_k, gla_v, gla_g, gla_zf, gla_zb,
                           gla_w_alpha[i], gla_b_alpha[i], gla_norm[i])
        y_attn = attn_branch(at_q, at_k, at_v, attn_q_norm[i], attn_k_norm[i])
        gates = jax.nn.sigmoid(h @ w_merge_gate[i] + b_merge_gate[i])
        gates = gates.reshape(bsz, seq_len, N_BRANCHES, D_MODEL)
        merged = (gates[:, :, 0] * (y_s5 @ w_branch_s5[i])
                  + gates[:, :, 1] * (y_gla @ w_branch_gla[i])
                  + gates[:, :, 2] * (y_attn @ w_branch_attn[i]))
        x = x + merged @ w_out[i]

        h = rms_norm(x, ffn2_norm[i])
        x = x + 0.5 * swiglu_ffn(h, ffn2_w_gate[i], ffn2_w_up[i], ffn2_w_down[i])
    return rms_norm(x, final_norm)
```

```python
import contextlib
import numpy as np
import ml_dtypes
import concourse.bass as bass
import concourse.mybir as mybir
from concourse.bass_utils import run_bass_kernel_spmd

F32 = mybir.dt.float32
BF16 = mybir.dt.bfloat16
AF = mybir.ActivationFunctionType
ALU = mybir.AluOpType
NPBF = ml_dtypes.bfloat16

D_MODEL = 1024
BATCH = 4
SEQ = 4096
DEPTH = 2
D_FF = 2816
EPS = 1e-6
NCORES = 8
TOK = 2048
KD = D_MODEL // 128
NFF = D_FF // 128

ENGINES = ("pe", "act", "dve", "pool", "sp")
import os
PROG_LIMIT = int(os.environ.get("PROG_LIMIT", "100000000"))
N_DMA_SEMS = 24


class Tok:
    __slots__ = ("name", "last_w", "readers", "excl")

    def __init__(self, name, excl=False):
        self.name = name
        self.last_w = None
        self.readers = []
        self.excl = excl


class Op:
    __slots__ = ("eng", "fn", "idx", "eidx", "is_dma", "waits", "signal", "clock", "dma_slot",
                 "dma_val", "dma_known", "prewait", "waited", "cc")


class Prog:
    def __init__(self):
        self.nc = bass.Bass("TRN2", target_bir_lowering=False)
        self.ops = []
        self.eng_ops = {e: [] for e in ENGINES}
        self.known = {e: {f: -1 for f in ENGINES} for e in ENGINES}
        self.known_dma = {e: set() for e in ENGINES}
        self.stack = contextlib.ExitStack()
        self.dma_count = 0
        self.dma_slot_last = [None] * N_DMA_SEMS
        self.ntok = 0
        self.drams = {}
        self.mega = None
        self.moff = 0
        self.banks = None
        self.ncc = 0

    def enable_mega(self, words):
        self.mega = self.stack.enter_context(self.nc.sbuf_tensor("mega", [128, words], F32))
        self.mega_words = words

    def phase_reset(self):
        self.moff = 0

    def sbuf(self, name, shape, dtype=F32):
        if self.mega is None:
            return self.stack.enter_context(self.nc.sbuf_tensor(name, list(shape), dtype))
        shape = list(shape)
        n = 1
        for d in shape[1:]:
            n *= d
        isz = 2 if dtype == BF16 else 4
        words = (n * isz + 3) // 4
        words = (words + 15) // 16 * 16
        assert self.moff + words <= self.mega_words, (name, self.moff, words)
        ap = self.mega[0:shape[0], self.moff:self.moff + words]
        self.moff += words
        if dtype != F32:
            ap = ap.bitcast(dtype)
        ap = ap[:, 0:n]
        if len(shape) > 2:
            names = " ".join(f"d{i}" for i in range(len(shape) - 1))
            kw = {f"d{i}": shape[i + 1] for i in range(len(shape) - 1)}
            ap = ap.rearrange(f"p ({names}) -> p {names}", **kw)
        return ap

    def psum(self, name, shape, dtype=F32):
        return self.stack.enter_context(self.nc.psum_tensor(name, list(shape), dtype))

    def psum_banks(self):
        if self.banks is None:
            bs = [self.psum(f"bank{i}", [128, 512], F32) for i in range(8)]
            ts = [self.tok(f"bank{i}", True) for i in range(8)]
            self.banks = (bs, ts)
        return self.banks

    def collective(self, src_ap, dst_ap, reads, writes):
        def fn(e):
            return e.collective_compute("AllGather", ALU.bypass, replica_groups=[[0, 1], [2, 3], [4, 5], [6, 7]],
                                        ins=[src_ap], outs=[dst_ap])
        return self._add("pool", fn, list(reads), list(writes), True, True)

    def dram(self, name, shape, dtype, kind):
        t = self.nc.dram_tensor(name, list(shape), dtype, kind=kind)
        self.drams[name] = t
        return t.ap()

    def tok(self, name=None, excl=False):
        self.ntok += 1
        return Tok(name or f"t{self.ntok}", excl)

    def _add(self, eng, fn, reads, writes, is_dma, is_cc=False):
        if len(self.ops) >= PROG_LIMIT:
            return None
        op = Op()
        op.cc = None
        op.eng = eng
        op.fn = fn
        op.idx = len(self.ops)
        op.eidx = len(self.eng_ops[eng])
        op.is_dma = is_dma
        op.signal = False
        op.waits = []
        op.prewait = None
        op.waited = False
        known = self.known[eng]
        kd = self.known_dma[eng]
        deps = set()
        for t in reads:
            if t.last_w is not None:
                deps.add(t.last_w)
        for t in writes:
            if t.last_w is not None:
                deps.add(t.last_w)
            for r in t.readers:
                deps.add(r)
        for d in sorted(deps, key=lambda o: -o.idx):
            if d is op:
                continue
            if d.is_dma:
                if d.idx in kd:
                    continue
                op.waits.append(d)
                d.signal = True
                d.waited = True
                kd.add(d.idx)
                kd |= d.dma_known
                for f in ENGINES:
                    if d.clock[f] > known[f]:
                        known[f] = d.clock[f]
            else:
                if d.eng == eng and not (t_is_raw(d, reads)):
                    continue
                if d.eidx <= known[d.eng]:
                    continue
                op.waits.append(d)
                d.signal = True
                for f in ENGINES:
                    if d.clock[f] > known[f]:
                        known[f] = d.clock[f]
                kd |= d.dma_known
                known[d.eng] = max(known[d.eng], d.eidx)
        if is_cc:
            op.cc = self.ncc
            self.ncc += 1
            op.signal = True
        elif is_dma:
            slot = self.dma_count % N_DMA_SEMS
            prev = self.dma_slot_last[slot]
            if prev is not None and prev.idx not in kd:
                op.prewait = prev
                prev.signal = True
                prev.waited = True
                kd.add(prev.idx)
            op.dma_slot = slot
            op.dma_val = 16 * (self.dma_count // N_DMA_SEMS + 1)
            self.dma_slot_last[slot] = op
            self.dma_count += 1
            op.signal = True
        op.clock = dict(known)
        if not is_dma:
            op.clock[eng] = max(op.clock[eng], -1)
        op.dma_known = set(kd)
        for t in reads:
            t.readers.append(op)
        for t in writes:
            t.last_w = op
            t.readers = []
        self.ops.append(op)
        self.eng_ops[eng].append(op)
        return op

    def op(self, eng, fn, reads=(), writes=()):
        reads, writes = list(reads), list(writes)
        for t in reads:
            if t.excl and t not in writes:
                writes.append(t)
        return self._add(eng, fn, reads, writes, False)

    def dma(self, eng, out, in_, reads=(), writes=()):
        def fn(e):
            return e.dma_start(out=out, in_=in_)
        return self._add(eng, fn, list(reads), list(writes), True)

    def barrier(self):
        lasts = []
        for e in ENGINES:
            nd = [o for o in self.eng_ops[e] if (not o.is_dma) and o.fn is not None]
            if nd:
                lasts.append(nd[-1])
        dmas = [o for o in self.ops if o.is_dma and not o.waited]
        for e in ENGINES:
            toks = []
            for l in lasts + dmas:
                tt = Tok("b")
                tt.last_w = l
                toks.append(tt)
            self._add(e, None, toks, [], False)

    def emit(self):
        nc = self.nc
        sems = {}
        for e in ENGINES:
            sems[e] = self.stack.enter_context(nc.semaphore("s_" + e))
        dsem = [self.stack.enter_context(nc.semaphore(f"d{i}")) for i in range(N_DMA_SEMS)]
        csem = [self.stack.enter_context(nc.semaphore(f"cc{i}")) for i in range(self.ncc)]
        for e in ENGINES:
            c = 0
            for o in self.eng_ops[e]:
                if o.is_dma:
                    continue
                if o.signal and o.fn is None:
                    raise RuntimeError("barrier op cannot signal")
                if o.signal:
                    c += 1
                o.dma_val = c if not o.is_dma else o.dma_val
        block = self.stack.enter_context(nc.Block())
        engmap = {"pe": block.tensor, "act": block.scalar, "dve": block.vector,
                  "pool": block.gpsimd, "sp": block.sync}

        def make(ename):
            def body(e):
                for o in self.eng_ops[ename]:
                    if o.prewait is not None:
                        e.wait_ge(dsem[o.prewait.dma_slot], o.prewait.dma_val)
                    for d in o.waits:
                        if d.cc is not None:
                            e.wait_ge(csem[d.cc], 1)
                        elif d.is_dma:
                            e.wait_ge(dsem[d.dma_slot], d.dma_val)
                        else:
                            e.wait_ge(sems[d.eng], d.dma_val)
                    if o.fn is None:
                        continue
                    ins = o.fn(e)
                    if o.cc is not None:
                        ins.then_inc(csem[o.cc])
                    elif o.is_dma:
                        ins.then_inc(dsem[o.dma_slot], 16)
                    elif o.signal:
                        ins.then_inc(sems[ename], 1)
            return body

        for ename in ENGINES:
            if self.eng_ops[ename]:
                engmap[ename](make(ename))
        self.stack.close()
        return nc


def t_is_raw(d, reads):
    for t in reads:
        if t.last_w is d:
            return True
    return False


class Ctx:
    pass


def setup_common(P, ntok=TOK):
    C = Ctx()
    C.P = P
    C.ntok = ntok
    C.nblk = ntok // 512
    C.x = P.sbuf("x", [128, KD, ntok], F32)
    C.xt = [P.tok(f"x{b}") for b in range(C.nblk)]
    C.hn = P.sbuf("hn", [128, KD, ntok], BF16)
    C.hnt = [P.tok(f"hn{b}") for b in range(C.nblk)]
    C.ones = P.sbuf("ones", [128, 128], F32)
    C.ones_t = P.tok("ones")
    P.op("pool", lambda e: e.memset(C.ones[:], 1.0), [], [C.ones_t])
    C.sq = [P.sbuf(f"sq{i}", [128, 512], F32) for i in range(2)]
    C.sq_t = [P.tok() for i in range(2)]
    C.arena = P.sbuf("arenaC", [128, 12 * 1024], F32)
    C.rstd = P.sbuf("rstd", [128, 512], F32)
    C.rstd_t = P.tok("rstd")
    bs, ts = P.psum_banks()
    C.pst, C.pst_t = bs[0], ts[0]
    C.pg, C.pg_t = bs[1:3], ts[1:3]
    C.pu, C.pu_t = bs[3:5], ts[3:5]
    C.po, C.po_t = bs[5:7], ts[5:7]
    C.cnt = 0
    return C


def load_vec(P, C, name, dram_ap, n):
    t = P.sbuf(name, [128, n], F32)
    tk = P.tok(name)
    P.dma("sp", t[:], dram_ap, [], [tk])
    return t, tk


def rmsnorm_block(P, C, b, gain, gain_t, out, out_t, out_is_f32=False):
    sl = slice(b * 512, (b + 1) * 512)
    for k in range(KD):
        sq, sq_t = C.sq[k % 2], C.sq_t[k % 2]
        P.op("act", lambda e, k=k, sq=sq: e.activation(out=sq[:], in_=C.x[:, k, sl], func=AF.Square),
             [C.xt[b]], [sq_t])
        P.op("pe", lambda e, k=k, sq=sq: e.matmul(C.pst[:], lhsT=C.ones[:], rhs=sq[:],
                                                 start=(k == 0), stop=(k == KD - 1)),
             [sq_t, C.ones_t], [C.pst_t])
    P.op("act", lambda e: e.activation(out=C.rstd[:], in_=C.pst[:], func=AF.Sqrt,
                                       scale=1.0 / D_MODEL, bias=C.epsb[:]),
         [C.pst_t, C.epsb_t], [C.rstd_t])
    P.op("dve", lambda e: e.reciprocal(out=C.rstd[:], in_=C.rstd[:]), [C.rstd_t], [C.rstd_t])
    for k in range(KD):
        P.op("dve", lambda e, k=k: e.scalar_tensor_tensor(
            out=out[:, k, sl], in0=C.x[:, k, sl], scalar=gain[:, k:k + 1], in1=C.rstd[:],
            op0=ALU.mult, op1=ALU.mult), [C.xt[b], gain_t, C.rstd_t], [out_t[b]])


def ffn_alloc(P, C, nsplit=2):
    nblk = C.nblk
    per = NFF // nsplit
    if True:
        C.act = C.arena[:, 0:per * C.ntok // 2].bitcast(BF16).rearrange("p (c t) -> p c t", c=per)
        C.act_t = [[P.tok() for b in range(nblk)] for c in range(per)]
        C.wst = [P.sbuf(f"wst{i}", [128, 2, KD, 128], F32) for i in range(2)]
        C.wst_t = [P.tok() for i in range(2)]
        C.wb = [P.sbuf(f"wb{i}", [128, 2, KD, 128], BF16) for i in range(2)]
        C.wb_t = [P.tok() for i in range(2)]
        C.wdst = [P.sbuf(f"wdst{i}", [128, per, 128], F32) for i in range(2)]
        C.wdst_t = [P.tok() for i in range(2)]
        C.wdb = [P.sbuf(f"wdb{i}", [128, per, 128], BF16) for i in range(2)]
        C.wdb_t = [P.tok() for i in range(2)]
        C.sg = [P.sbuf(f"sg{i}", [128, 512], F32) for i in range(2)]
        C.sg_t = [P.tok() for i in range(2)]
        C.wcnt = 0
        C.wdcnt = 0
        C.gcnt = 0
        C.ocnt = 0


def ffn(P, C, wg_d, wu_d, wd_d, nsplit=2):
    nblk = C.nblk
    per = NFF // nsplit
    if not hasattr(C, "act"):
        ffn_alloc(P, C, nsplit)
    for h in range(nsplit):
        for cl in range(per):
            c = h * per + cl
            i = C.wcnt % 2
            C.wcnt += 1
            P.dma("sp", C.wst[i][:, 0], wg_d[c], [], [C.wst_t[i]])
            P.dma("sp", C.wst[i][:, 1], wu_d[c], [], [C.wst_t[i]])
            P.op("pool", lambda e, i=i: e.tensor_copy(out=C.wb[i][:], in_=C.wst[i][:]),
                 [C.wst_t[i]], [C.wb_t[i]])
            for b in range(nblk):
                sl = slice(b * 512, (b + 1) * 512)
                j = C.gcnt % 2
                C.gcnt += 1
                for k in range(KD):
                    P.op("pe", lambda e, i=i, j=j, k=k, sl=sl: e.matmul(
                        C.pg[j][:], lhsT=C.wb[i][:, 0, k, :], rhs=C.hn[:, k, sl],
                        start=(k == 0), stop=(k == KD - 1)), [C.wb_t[i], C.hnt[b]], [C.pg_t[j]])
                for k in range(KD):
                    P.op("pe", lambda e, i=i, j=j, k=k, sl=sl: e.matmul(
                        C.pu[j][:], lhsT=C.wb[i][:, 1, k, :], rhs=C.hn[:, k, sl],
                        start=(k == 0), stop=(k == KD - 1)), [C.wb_t[i], C.hnt[b]], [C.pu_t[j]])
                P.op("act", lambda e, j=j: e.activation(out=C.sg[j][:], in_=C.pg[j][:], func=AF.Silu),
                     [C.pg_t[j]], [C.sg_t[j]])
                P.op("dve", lambda e, j=j, cl=cl, sl=sl: e.tensor_tensor(
                    out=C.act[:, cl, sl], in0=C.sg[j][:], in1=C.pu[j][:], op=ALU.mult),
                    [C.sg_t[j], C.pu_t[j]], [C.act_t[cl][b]])
        for m in range(KD):
            i = C.wdcnt % 2
            C.wdcnt += 1
            P.dma("sp", C.wdst[i][:], wd_d[m, :, h * per:(h + 1) * per, :], [], [C.wdst_t[i]])
            P.op("pool", lambda e, i=i: e.tensor_copy(out=C.wdb[i][:], in_=C.wdst[i][:]),
                 [C.wdst_t[i]], [C.wdb_t[i]])
            for b in range(nblk):
                sl = slice(b * 512, (b + 1) * 512)
                j = C.ocnt % 2
                C.ocnt += 1
                for cl in range(per):
                    P.op("pe", lambda e, i=i, j=j, cl=cl, sl=sl: e.matmul(
                        C.po[j][:], lhsT=C.wdb[i][:, cl, :], rhs=C.act[:, cl, sl],
                        start=(cl == 0), stop=(cl == per - 1)),
                        [C.wdb_t[i], C.act_t[cl][b]], [C.po_t[j]])
                P.op("dve", lambda e, j=j, m=m, sl=sl: e.scalar_tensor_tensor(
                    out=C.x[:, m, sl], in0=C.po[j][:], scalar=0.5, in1=C.x[:, m, sl],
                    op0=ALU.mult, op1=ALU.add), [C.po_t[j], C.xt[b]], [C.xt[b]])


def load_x(P, C, x_d):
    for k in range(KD):
        for b in range(C.nblk):
            sl = slice(b * 512, (b + 1) * 512)
            P.dma("sp", C.x[:, k, sl], x_d[k, :, sl], [], [C.xt[b]])


def store_feat(P, C, src, src_t, dst_d, eng="pool"):
    toks = []
    for k in range(KD):
        t = P.tok()
        P.dma(eng, dst_d[k], src[:, k, :], list(src_t), [t])
        toks.append(t)
    return toks


def eps_const(P, C):
    C.epsb = P.sbuf("epsb", [128, 1], F32)
    C.epsb_t = P.tok("epsb")
    P.op("pool", lambda e: e.memset(C.epsb[:], EPS), [], [C.epsb_t])


def build_A():
    P = Prog()
    if os.environ.get("MEGA_A"):
        P.enable_mega(MEGA_WORDS)
    for i in range(int(os.environ.get("EXTRA_IN", "0"))):
        P.dram(f"extra{i}", [128, 2], F32, "ExternalInput")
    for i in range(int(os.environ.get("EXTRA_INT", "0"))):
        P.dram(f"extraint{i}", [128, 2048], BF16, "Internal")
    x_d = P.dram("xT", [KD, 128, TOK], F32, "ExternalInput")
    wg_d = P.dram("wg", [NFF, 128, KD, 128], F32, "ExternalInput")
    wu_d = P.dram("wu", [NFF, 128, KD, 128], F32, "ExternalInput")
    wd_d = P.dram("wd", [KD, 128, NFF, 128], F32, "ExternalInput")
    g1_d = P.dram("g_ffn", [128, KD], F32, "ExternalInput")
    g2_d = P.dram("g_mix", [128, KD], F32, "ExternalInput")
    xo_d = P.dram("xo", [KD, 128, TOK], F32, "ExternalOutput")
    hn_d = P.dram("hno", [KD, 128, TOK], BF16, "ExternalOutput")
    C = setup_common(P)
    eps_const(P, C)
    g1, g1t = load_vec(P, C, "g1", g1_d, KD)
    g2, g2t = load_vec(P, C, "g2", g2_d, KD)
    load_x(P, C, x_d)
    for b in range(C.nblk):
        rmsnorm_block(P, C, b, g1, g1t, C.hn, C.hnt)
    ffn(P, C, wg_d, wu_d, wd_d)
    for b in range(C.nblk):
        rmsnorm_block(P, C, b, g2, g2t, C.hn, C.hnt)
    store_feat(P, C, C.x, C.xt, xo_d)
    store_feat(P, C, C.hn, C.hnt, hn_d)
    P.barrier()
    return P.emit()


def feat_major(a):
    t = np.ascontiguousarray(a.T)
    return t.reshape(t.shape[0] // 128, 128, t.shape[1])


def tile_w_kc(w):
    K, N = w.shape
    return np.ascontiguousarray(w.reshape(K // 128, 128, N // 128, 128).transpose(2, 1, 0, 3))


def vec128(v):
    return np.ascontiguousarray(v.reshape(-1, 128).T)


NB = SEQ // 512
NT = SEQ // 128
ARENA_W = 27 * 1024
WCOLS = 1056


class Arena:
    def __init__(self, P):
        self.t = P.sbuf("arena", [128, ARENA_W], F32)
        if P.mega is not None:
            self.t = self.t
        self.off = 0

    def reset(self):
        self.off = 0

    def take(self, nbytes, dtype, shape=None, parts=128):
        words = (nbytes + 3) // 4
        assert self.off + words <= ARENA_W, (self.off, words)
        ap = self.t[0:parts, self.off:self.off + words]
        self.off += words
        if dtype != F32:
            ap = ap.bitcast(dtype)
        return ap


def load_weights(P, B, w_d, ncols):
    c0 = 0
    while c0 < ncols:
        n = min(256, ncols - c0)
        i = B.wcnt % 2
        B.wcnt += 1
        P.dma("sp", B.wst[i][:, :, 0:n], w_d[:, :, c0:c0 + n], [], [B.wst_t[i]])
        P.op("pool", lambda e, i=i, n=n, c0=c0: e.tensor_copy(out=B.wbf[:, :, c0:c0 + n],
                                                             in_=B.wst[i][:, :, 0:n]),
             [B.wst_t[i]], [B.wbf_t])
        c0 += n


def proj_feat(P, B, col0, m, blk, ps, ps_t, n=512, tok0=None):
    t0 = blk * 512 if tok0 is None else tok0
    for k in range(KD):
        P.op("pe", lambda e, k=k: e.matmul(ps[0:m, 0:n], lhsT=B.wbf[:, k, col0:col0 + m],
                                         rhs=B.hn[:, k, t0:t0 + n],
                                         start=(k == 0), stop=(k == KD - 1)),
             [B.wbf_t, B.hn_t], [ps_t])


def proj_tok(P, B, col0, n, tile, ps, ps_t):
    t0 = tile * 128
    for k in range(KD):
        P.op("pe", lambda e, k=k: e.matmul(ps[:, 0:n], lhsT=B.hn[:, k, t0:t0 + 128],
                                         rhs=B.wbf[:, k, col0:col0 + n],
                                         start=(k == 0), stop=(k == KD - 1)),
             [B.wbf_t, B.hn_t], [ps_t])


def setup_B(P):
    B = Ctx()
    B.P = P
    B.hn = P.sbuf("hnB", [128, KD, SEQ], BF16)
    B.hn_t = P.tok("hnB")
    B.wst = [P.sbuf(f"wstB{i}", [128, KD, 256], F32) for i in range(2)]
    B.wst_t = [P.tok() for i in range(2)]
    B.wbf = P.sbuf("wbfB", [128, KD, WCOLS], BF16)
    B.wbf_t = P.tok("wbf")
    B.wcnt = 0
    B.yout_t = P.tok("yout")
    B.ar = Arena(P)
    B.ps, B.ps_t = P.psum_banks()
    B.ones = P.sbuf("onesB", [128, 128], F32)
    B.ones_t = P.tok("onesB")
    P.op("pool", lambda e: e.memset(B.ones[:], 1.0), [], [B.ones_t])
    B.ones2 = P.sbuf("ones2B", [128, 128], F32)
    B.ones2_t = P.tok("ones2B")
    P.op("pool", lambda e: e.memset(B.ones2[:], 0.0), [], [B.ones2_t])
    P.op("pool", lambda e: e.memset(B.ones2[0:64, 0:64], 1.0), [], [B.ones2_t])
    P.op("pool", lambda e: e.memset(B.ones2[64:128, 64:128], 1.0), [], [B.ones2_t])
    B.epsb = P.sbuf("epsB", [128, 2], F32)
    B.epsb_t = P.tok("epsB")
    P.op("pool", lambda e: e.memset(B.epsb[:, 0:1], EPS), [], [B.epsb_t])
    P.op("pool", lambda e: e.memset(B.epsb[:, 1:2], 64.0 * EPS), [], [B.epsb_t])
    B.oneb = P.sbuf("oneB", [128, 1], F32)
    B.oneb_t = P.tok("oneB")
    P.op("pool", lambda e: e.memset(B.oneb[:], 1.0), [], [B.oneb_t])
    return B


def attn_branch(P, B, w_d, cos_d, sin_d, gq_d, gk_d, out_d):
    ar = B.ar
    ar.reset()
    cos = ar.take(SEQ * 4, F32)
    sin = ar.take(SEQ * 4, F32)
    QT = ar.take(2 * SEQ * 2, BF16).rearrange("p (a t) -> p a t", a=2)
    KT = ar.take(SEQ * 2, BF16)
    VA = [ar.take(NT * 128 * 2, BF16).rearrange("p (n c) -> p n c", c=128) for _ in range(2)]
    YA = ar.take(2 * SEQ * 2, BF16).rearrange("p (a t) -> p a t", a=2)
    PT = [ar.take(512 * 2, BF16) for _ in range(3)]
    sq = ar.take(512 * 4, F32)
    rr = ar.take(512 * 4, F32)
    t1 = ar.take(512 * 4, F32)
    t2 = ar.take(512 * 4, F32)
    rc = ar.take(512 * 4, F32)
    gq = ar.take(8, F32)
    gk = ar.take(8, F32)
    tk = lambda n: P.tok(n)
    cos_t, sin_t, QT_t, KT_t, YA_t = tk("cos"), tk("sin"), tk("QT"), tk("KT"), tk("YA")
    VA_t = [tk("va0"), tk("va1")]
    PT_t = [tk("pt") for _ in range(3)]
    sq_t, rr_t, t1_t, t2_t, rc_t, gq_t, gk_t = (tk(n) for n in ("sq", "rr", "t1", "t2", "rc", "gq", "gk"))
    P.dma("sp", cos, cos_d, [], [cos_t])
    P.dma("sp", sin, sin_d, [], [sin_t])
    P.dma("sp", gq, gq_d, [], [gq_t])
    P.dma("sp", gk, gk_d, [], [gk_t])
    load_weights(P, B, w_d, 832)
    P.op("pool", lambda e: e.memset(VA[0][:, :, 64:128], 1.0), [], [VA_t[0]])
    P.op("pool", lambda e: e.memset(VA[1][:, :, 0:64], 1.0), [], [VA_t[1]])
    psA, psA_t = B.ps, B.ps_t

    def qk_prep(col, colrot, g, g_t, dst, dst_t, is_q, blk):
        sl = slice(blk * 512, (blk + 1) * 512)
        pa, pb, pc = psA[0], psA[1], psA[2]
        proj_feat(P, B, col, 128, blk, pa, psA_t[0])
        proj_feat(P, B, colrot, 128, blk, pb, psA_t[1])
        P.op("act", lambda e: e.activation(out=sq, in_=pa[:], func=AF.Square), [psA_t[0]], [sq_t])
        P.op("pe", lambda e: e.matmul(pc[:], lhsT=B.ones2[:], rhs=sq, start=True, stop=True),
             [sq_t, B.ones2_t], [psA_t[2]])
        if is_q:
            P.op("act", lambda e: e.activation(out=rr, in_=pc[:], func=AF.Sqrt, scale=1.0,
                                               bias=B.epsb[:, 1:2]), [psA_t[2], B.epsb_t], [rr_t])
        else:
            P.op("act", lambda e: e.activation(out=rr, in_=pc[:], func=AF.Sqrt, scale=1.0 / 64,
                                               bias=B.epsb[:, 0:1]), [psA_t[2], B.epsb_t], [rr_t])
        P.op("dve", lambda e: e.reciprocal(out=rr, in_=rr), [rr_t], [rr_t])
        P.op("dve", lambda e: e.scalar_tensor_tensor(out=t1, in0=pa[:], scalar=g[:, 0:1],
                                                     in1=cos[:, sl], op0=ALU.mult, op1=ALU.mult),
             [psA_t[0], g_t, cos_t], [t1_t])
        P.op("dve", lambda e: e.scalar_tensor_tensor(out=t2, in0=pb[:], scalar=g[:, 1:2],
                                                     in1=sin[:, sl], op0=ALU.mult, op1=ALU.mult),
             [psA_t[1], g_t, sin_t], [t2_t])
        P.op("dve", lambda e: e.tensor_tensor(out=t1, in0=t1, in1=t2, op=ALU.add), [t1_t, t2_t], [t1_t])
        P.op("dve", lambda e: e.tensor_tensor(out=dst[:, sl], in0=t1, in1=rr, op=ALU.mult),
             [t1_t, rr_t], [dst_t])

    for blk in range(NB):
        for pair in range(2):
            qk_prep(pair * 128, 256 + pair * 128, gq, gq_t, QT[:, pair, :], QT_t, True, blk)
        qk_prep(512, 640, gk, gk_t, KT, KT_t, False, blk)
    for tile in range(NT):
        pv = psA[3 + tile % 2]
        pv_t = psA_t[3 + tile % 2]
        proj_tok(P, B, 768, 64, tile, pv, pv_t)
        P.op("act", lambda e, tile=tile, pv=pv: e.copy(out=VA[0][:, tile, 0:64], in_=pv[:, 0:64]),
             [pv_t], [VA_t[0]])
        P.op("dve", lambda e, tile=tile, pv=pv: e.tensor_copy(out=VA[1][:, tile, 64:128], in_=pv[:, 0:64]),
             [pv_t], [VA_t[1]])
    po = [psA[6], psA[7]]
    po_t = [psA_t[6], psA_t[7]]
    def do_block(pair, j, qb, o, o_t):
        rows = slice(64 * j, 64 * j + 64)
        orow = slice(64 * (1 - j), 64 * (1 - j) + 64)
        qs = slice(qb * 512, (qb + 1) * 512)

        def qk(kt):
            s = psA[kt % 3]
            P.op("pe", lambda e: e.matmul(s[:], lhsT=KT[rows, kt * 128:(kt + 1) * 128],
                                          rhs=QT[rows, pair, qs], start=True, stop=True),
                 [KT_t, QT_t], [psA_t[kt % 3]])

        def step(kt):
            s = psA[kt % 3]
            pt = PT[kt % 3]
            P.op("act", lambda e: e.activation(out=pt, in_=s[:], func=AF.Exp),
                 [psA_t[kt % 3]], [PT_t[kt % 3]])
            if kt + 2 < NT:
                qk(kt + 2)
            P.op("pe", lambda e: e.matmul(o[:], lhsT=VA[j][:, kt, :], rhs=pt,
                                          start=(kt == 0), stop=(kt == NT - 1)),
                 [VA_t[j], PT_t[kt % 3]], [o_t])

        qk(0)
        qk(1)
        for kt in range(NT):
            step(kt)
        P.op("act", lambda e: e.copy(out=rc[rows, :], in_=o[orow, :]), [o_t], [rc_t])
        P.op("dve", lambda e: e.reciprocal(out=rc[rows, :], in_=rc[rows, :]), [rc_t], [rc_t])
        P.op("dve", lambda e: e.tensor_tensor(out=YA[rows, pair, qs], in0=o[rows, :],
                                              in1=rc[rows, :], op=ALU.mult),
             [o_t, rc_t], [YA_t])

    it = 0
    import os
    stage = int(os.environ.get("ATTN_STAGE", "9"))
    for pair in range(2):
        for j in range(2):
            for qb in range(NB):
                if stage == 0 or (stage == 1 and it >= 1):
                    continue
                do_block(pair, j, qb, po[it % 2], po_t[it % 2])
                it += 1
    for pair in range(2):
        for h in range(2):
            P.dma("pool", out_d[pair][h], YA[:, pair, h * TOK:(h + 1) * TOK], [YA_t], [B.yout_t])


def gla_branch(P, B, w_d, wal_d, nb_d, gn_d, cst_d, out_d):
    ar = B.ar
    ps, ps_t = B.ps, B.ps_t
    load_weights(P, B, w_d, 1056)
    tk = lambda n: P.tok(n)
    def one_head(hh):
        ar.reset()
        qd = [ar.take(SEQ * 2, BF16) for _ in range(2)]
        kd = [ar.take(SEQ * 2, BF16) for _ in range(2)]
        Sb = ar.take(2 * NT * 128 * 2, BF16).rearrange("p (d n e) -> p d n e", d=2, n=NT)
        vt = ar.take(NT * 128 * 2, BF16).rearrange("p (n e) -> p n e", e=128)
        GS = ar.take(SEQ * 2, BF16)
        YG = ar.take(SEQ * 2, BF16)
        tmp = [ar.take(512 * 4, F32) for _ in range(8)]
        e1, sp, Bc, bb, Eq, Ek, ysq, yt = tmp
        zT = ar.take(512 * 2, BF16)
        S = ar.take(128 * 4, F32)
        kvs = ar.take(128 * 4, F32)
        kdt = ar.take(128 * 2, BF16)
        Pm = [ar.take(512 * 2, BF16) for _ in range(2)]
        dec = ar.take(2 * NT * 4, F32).rearrange("p (d n) -> p d n", d=2)
        m01 = ar.take(512 * 4, F32)
        cst = ar.take(3 * 128 * 4, F32).rearrange("p (a c) -> p a c", a=3)
        identb = ar.take(128 * 2, BF16)
        wal = ar.take(2 * 256 * 4, F32).rearrange("p (d c) -> p d c", d=2)
        walb = ar.take(2 * 256 * 2, BF16).rearrange("p (d c) -> p d c", d=2)
        ar.take(64, F32)
        nb = ar.take(64, F32)[:, 0:4].rearrange("p (d h) -> p d h", d=2)
        gn = ar.take(64, F32)[:, 0:1]
        ar.take(64, F32)
        rr = ar.take(512 * 4, F32)
        qd_t = [tk("qd0"), tk("qd1")]
        kd_t = [tk("kd0"), tk("kd1")]
        Sb_t = [tk("sb0"), tk("sb1")]
        (vt_t, GS_t, YG_t, e1_t, sp_t, Bc_t, bb_t, Eq_t, Ek_t, ysq_t, yt_t, zT_t, S_t, kvs_t, kdt_t,
         dec_t, m01_t, cst_t, identb_t, wal_t, walb_t, nb_t, gn_t, rr_t) = (tk("g") for _ in range(24))
        Pm_t = [tk("pm0"), tk("pm1")]
        P.dma("sp", cst, cst_d, [], [cst_t])
        P.dma("sp", wal[0:32], wal_d, [], [wal_t])
        P.dma("sp", nb, nb_d, [], [nb_t])
        P.dma("sp", gn, gn_d, [], [gn_t])
        small = [cst_t, wal_t, nb_t, gn_t]
        P.op("pool", lambda e: e.tensor_copy(out=walb[0:32], in_=wal[0:32]), small, [walb_t])
        P.op("pool", lambda e: e.tensor_copy(out=identb, in_=cst[:, 2, :]), small, [identb_t])
        P.op("dve", lambda e: e.tensor_scalar(out=nb, in0=nb, scalar1=-1.0, scalar2=None, op0=ALU.mult),
             small, [nb_t])
        P.op("pool", lambda e: e.memset(m01, 1.0), [], [m01_t])
        P.op("pool", lambda e: e.memset(m01.rearrange("p (c i) -> p c i", i=128)[:, :, 0:1], 0.0),
             [], [m01_t])
        P.op("pool", lambda e: e.memset(S, 0.0), [], [S_t])
        scale_q = 128.0 ** -0.5

        def prep_blk(blk):
            sl = slice(blk * 512, (blk + 1) * 512)
            pz, pz_t = ps[0], ps_t[0]
            proj_feat(P, B, 1024, 32, blk, pz, pz_t)
            P.op("act", lambda e: e.copy(out=zT[0:32], in_=pz[0:32, :]), [pz_t], [zT_t])
            pq, pq_t = ps[1], ps_t[1]
            pk, pk_t = ps[2], ps_t[2]
            pgt, pgt_t = ps[3], ps_t[3]
            proj_feat(P, B, 0 + hh * 128, 128, blk, pq, pq_t)
            proj_feat(P, B, 256 + hh * 128, 128, blk, pk, pk_t)
            proj_feat(P, B, 768 + hh * 128, 128, blk, pgt, pgt_t)
            P.op("act", lambda e: e.activation(out=GS[:, sl], in_=pgt[:], func=AF.Silu), [pgt_t], [GS_t])
            def prep_dir(d):
                pl, pl_t = ps[4 + d], ps_t[4 + d]
                P.op("pe", lambda e: e.matmul(pl[:], lhsT=walb[0:32, d, hh * 128:(hh + 1) * 128],
                                              rhs=zT[0:32], start=True, stop=True),
                     [walb_t, zT_t], [pl_t])
                P.op("act", lambda e: e.activation(out=e1, in_=pl[:], func=AF.Exp, scale=-1.0,
                                                   bias=nb[:, d, hh:hh + 1]), [pl_t, nb_t], [e1_t])
                P.op("act", lambda e: e.activation(out=sp, in_=e1, func=AF.Ln, scale=1.0, bias=B.oneb[:]),
                     [e1_t, B.oneb_t], [sp_t])
                P.op("dve", lambda e: e.tensor_tensor_scan(out=Bc, data0=m01, data1=sp, initial=0.0,
                                                           op0=ALU.mult, op1=ALU.add),
                     [m01_t, sp_t], [Bc_t])
                if d == 0:
                    src, src_t = Bc, Bc_t
                else:
                    P.op("dve", lambda e: e.tensor_tensor(out=bb, in0=sp, in1=Bc, op=ALU.subtract),
                         [sp_t, Bc_t], [bb_t])
                    b3 = bb.rearrange("p (c i) -> p c i", i=128)
                    tot = Bc.rearrange("p (c i) -> p c i", i=128)[:, :, 127:128]
                    P.op("dve", lambda e: e.tensor_tensor(out=b3, in0=b3, in1=tot.to_broadcast([128, 4, 128]),
                                                          op=ALU.add), [bb_t, Bc_t], [bb_t])
                    src, src_t = bb, bb_t
                P.op("act", lambda e: e.activation(out=Eq, in_=src, func=AF.Exp, scale=-1.0 / 16),
                     [src_t], [Eq_t])
                P.op("act", lambda e: e.activation(out=Ek, in_=src, func=AF.Exp, scale=1.0 / 16),
                     [src_t], [Ek_t])
                P.op("dve", lambda e: e.scalar_tensor_tensor(out=qd[d][:, sl], in0=pq[:], scalar=scale_q,
                                                             in1=Eq, op0=ALU.mult, op1=ALU.mult),
                     [pq_t, Eq_t], [qd_t[d]])
                P.op("dve", lambda e: e.tensor_tensor(out=kd[d][:, sl], in0=pk[:], in1=Ek, op=ALU.mult),
                     [pk_t, Ek_t], [kd_t[d]])
                col = 127 if d == 0 else 0
                e3 = Eq.rearrange("p (c i) -> p c i", i=128)[:, :, col]
                P.op("dve", lambda e: e.tensor_copy(out=dec[:, d, blk * 4:(blk + 1) * 4], in_=e3),
                     [Eq_t], [dec_t])

            prep_dir(0)
            prep_dir(1)

        for blk in range(NB):
            prep_blk(blk)

        def v_tile(tile):
            pv, pv_t = ps[6 + tile % 2], ps_t[6 + tile % 2]
            proj_tok(P, B, 512 + hh * 128, 128, tile, pv, pv_t)
            P.op("act", lambda e: e.copy(out=vt[:, tile, :], in_=pv[:, 0:128]), [pv_t], [vt_t])

        for tile in range(NT):
            v_tile(tile)

        def state_step(d, n, k):
            ptr = ps[0 + k % 2][:, 0:64].bitcast(BF16)
            ptr_t = ps_t[0 + k % 2]
            pkv, pkv_t = ps[2 + k % 2], ps_t[2 + k % 2]
            cs = slice(n * 128, (n + 1) * 128)
            P.op("pe", lambda e: e.transpose(ptr, kd[d][:, cs], identb), [kd_t[d], identb_t], [ptr_t])
            P.op("dve", lambda e: e.tensor_copy(out=kdt, in_=ptr), [ptr_t], [kdt_t])
            P.op("pe", lambda e: e.matmul(pkv[:, 0:128], lhsT=kdt, rhs=vt[:, n, :], start=True, stop=True),
                 [kdt_t, vt_t], [pkv_t])
            P.op("act", lambda e: e.copy(out=Sb[:, d, n, :], in_=S), [S_t], [Sb_t[d]])
            P.op("dve", lambda e: e.tensor_scalar(out=kvs, in0=pkv[:, 0:128], scalar1=dec[:, d, n:n + 1],
                                                  scalar2=None, op0=ALU.mult), [pkv_t, dec_t], [kvs_t])
            P.op("dve", lambda e: e.scalar_tensor_tensor(out=S, in0=S, scalar=dec[:, d, n:n + 1], in1=kvs,
                                                         op0=ALU.mult, op1=ALU.add),
                 [S_t, dec_t, kvs_t], [S_t])

        k = 0
        for n in range(NT):
            state_step(0, n, k)
            k += 1
        P.op("pool", lambda e: e.memset(S, 0.0), [S_t], [S_t])
        for n in range(NT - 1, -1, -1):
            state_step(1, n, k)
            k += 1

        def out_blk(blk):
            sl = slice(blk * 512, (blk + 1) * 512)
            psc = [ps[4], ps[5]]
            psc_t = [ps_t[4], ps_t[5]]
            po, po_t = ps[6 + blk % 2], ps_t[6 + blk % 2]
            for d in range(2):
                for c in range(4):
                    n = blk * 4 + c
                    cs = slice(n * 128, (n + 1) * 128)
                    lc = slice(c * 128, (c + 1) * 128)
                    P.op("pe", lambda e, d=d, cs=cs, lc=lc: e.matmul(
                        psc[d][:, lc], lhsT=kd[d][:, cs], rhs=qd[d][:, cs], start=True, stop=True),
                        [kd_t[d], qd_t[d]], [psc_t[d]])
                for c in range(4):
                    lc = slice(c * 128, (c + 1) * 128)
                    P.op("dve", lambda e, d=d, lc=lc: e.tensor_tensor(
                        out=Pm[d][:, lc], in0=psc[d][:, lc], in1=cst[:, d, :], op=ALU.mult),
                        [psc_t[d], cst_t], [Pm_t[d]])
            for c in range(4):
                n = blk * 4 + c
                cs = slice(n * 128, (n + 1) * 128)
                lc = slice(c * 128, (c + 1) * 128)
                P.op("pe", lambda e, n=n, lc=lc: e.matmul(po[:, lc], lhsT=vt[:, n, :], rhs=Pm[0][:, lc],
                                                          start=True, stop=False),
                     [vt_t, Pm_t[0]], [po_t])
                P.op("pe", lambda e, n=n, lc=lc, cs=cs: e.matmul(po[:, lc], lhsT=Sb[:, 0, n, :],
                                                                 rhs=qd[0][:, cs], start=False, stop=False),
                     [Sb_t[0], qd_t[0]], [po_t])
                P.op("pe", lambda e, n=n, lc=lc: e.matmul(po[:, lc], lhsT=vt[:, n, :], rhs=Pm[1][:, lc],
                                                          start=False, stop=False),
                     [vt_t, Pm_t[1]], [po_t])
                P.op("pe", lambda e, n=n, lc=lc, cs=cs: e.matmul(po[:, lc], lhsT=Sb[:, 1, n, :],
                                                                 rhs=qd[1][:, cs], start=False, stop=True),
                     [Sb_t[1], qd_t[1]], [po_t])
            pst, pst_t = ps[0 + blk % 2], ps_t[0 + blk % 2]
            P.op("act", lambda e: e.activation(out=ysq, in_=po[:], func=AF.Square), [po_t], [ysq_t])
            P.op("pe", lambda e: e.matmul(pst[:], lhsT=B.ones[:], rhs=ysq, start=True, stop=True),
                 [ysq_t, B.ones_t], [pst_t])
            P.op("act", lambda e: e.activation(out=rr, in_=pst[:], func=AF.Sqrt, scale=1.0 / 128,
                                               bias=B.epsb[:, 0:1]), [pst_t, B.epsb_t], [rr_t])
            P.op("dve", lambda e: e.reciprocal(out=rr, in_=rr), [rr_t], [rr_t])
            P.op("dve", lambda e: e.scalar_tensor_tensor(out=yt, in0=po[:], scalar=gn[:, 0:1], in1=rr,
                                                         op0=ALU.mult, op1=ALU.mult),
                 [po_t, gn_t, rr_t], [yt_t])
            P.op("dve", lambda e: e.tensor_tensor(out=YG[:, sl], in0=yt, in1=GS[:, sl], op=ALU.mult),
                 [yt_t, GS_t], [YG_t])

        for blk in range(NB):
            out_blk(blk)
        for h in range(2):
            P.dma("pool", out_d[hh][h], YG[:, h * TOK:(h + 1) * TOK], [YG_t], [B.yout_t])
        P.barrier()

    for head_i in range(2):
        one_head(head_i)


NKS = 12
TWO_PI = 6.283185307179586


def s5_branch(P, B, w_d, bblk_d, prm_d, c_d, dsk_d, out_d):
    ar = B.ar
    ar.reset()
    ps, ps_t = B.ps, B.ps_t
    tk = lambda n: P.tok(n)
    NF = 16
    ubf = ar.take(2 * SEQ * 2, BF16).rearrange("p (a t) -> p a t", a=2)
    Y32 = ar.take(2 * SEQ * 4, F32).rearrange("p (a t) -> p a t", a=2)
    bst = ar.take(16 * 128 * 4, F32)
    Bb = ar.take(16 * 128 * 2, BF16).rearrange("p (t d r g c) -> p t d r g c", t=2, d=2, r=2, g=2)
    prm = ar.take(NF * 3 * 4, F32).rearrange("p (f c) -> p f c", c=3)
    Cin = ar.take(NF * 2 * 16 * 4, F32).rearrange("p (f r h) -> p f r h", r=2, h=16)
    Cp = ar.take(NF * 2 * 16 * 4, F32).rearrange("p (f r h) -> p f r h", r=2, h=16)
    Cblk = ar.take(NF * 2 * 128 * 4, F32).rearrange("p (f r c) -> p f r c", r=2, c=128)
    LAM = ar.take(NKS * 3 * NF * 4, F32).rearrange("p (k c f) -> p k c f", k=NKS, c=3)
    sm = [ar.take(NF * 4, F32) for _ in range(12)]
    smi = ar.take(NF * 4, mybir.dt.int32)
    ctmp = [ar.take(NF * 16 * 4, F32).rearrange("p (f h) -> p f h", h=16) for _ in range(2)]
    dsk = ar.take(8, F32)
    ytmp = ar.take(512 * 4, F32)
    YS = ar.take(2 * SEQ * 2, BF16).rearrange("p (a t) -> p a t", a=2)
    (ubf_t, Y_t, bst_t, Bb_t, prm_t, Cin_t, Cp_t, Cblk_t, LAM_t, sm_t, dsk_t, ytmp_t, YS_t) = (
        tk("s5") for _ in range(13))
    load_weights(P, B, w_d, 256)
    P.dma("sp", bst, bblk_d.rearrange("p t d r g c -> p (t d r g c)"), [], [bst_t])
    P.dma("sp", prm, prm_d, [], [prm_t])
    P.dma("sp", Cin, c_d, [], [Cin_t])
    P.dma("sp", dsk, dsk_d, [], [dsk_t])
    P.op("pool", lambda e: e.tensor_copy(out=Bb.rearrange("p t d r g c -> p (t d r g c)"), in_=bst),
         [bst_t], [Bb_t])
    P.op("pool", lambda e: e.memset(Y32, 0.0), [], [Y_t])
    P.op("pool", lambda e: e.memset(Cblk, 0.0), [], [Cblk_t])

    def V(fn, reads=(), eng="dve"):
        P.op(eng, fn, [sm_t] + list(reads), [sm_t])

    lre, lim, ldt = prm[:, :, 0], prm[:, :, 1], prm[:, :, 2]
    dt, mag, ang, u_, nf, r_, m_, sn, cs_, t1, t2, inv = sm
    V(lambda e: e.activation(out=dt, in_=ldt, func=AF.Exp), [prm_t], "act")
    V(lambda e: e.tensor_tensor(out=t1, in0=lre, in1=dt, op=ALU.mult), [prm_t])
    V(lambda e: e.activation(out=mag, in_=t1, func=AF.Exp), [], "act")
    V(lambda e: e.tensor_tensor(out=ang, in0=lim, in1=dt, op=ALU.mult), [prm_t])

    def sin_of(dst, shift):
        V(lambda e: e.tensor_scalar(out=u_, in0=ang, scalar1=shift, scalar2=1.0 / TWO_PI,
                                    op0=ALU.add, op1=ALU.mult))
        V(lambda e: e.tensor_copy(out=smi, in_=u_))
        V(lambda e: e.tensor_copy(out=nf, in_=smi))
        V(lambda e: e.tensor_tensor(out=r_, in0=u_, in1=nf, op=ALU.subtract))
        V(lambda e: e.tensor_single_scalar(out=m_, in_=r_, scalar=0.5, op=ALU.is_gt))
        V(lambda e: e.tensor_tensor(out=r_, in0=r_, in1=m_, op=ALU.subtract))
        V(lambda e: e.tensor_single_scalar(out=m_, in_=r_, scalar=-0.5, op=ALU.is_lt))
        V(lambda e: e.tensor_tensor(out=r_, in0=r_, in1=m_, op=ALU.add))
        V(lambda e: e.activation(out=dst, in_=r_, func=AF.Sin, scale=TWO_PI), [], "act")

    sin_of(sn, 0.0)
    sin_of(cs_, TWO_PI / 4)
    L0re, L0im, L0nim = LAM[:, 0, 0, :], LAM[:, 0, 1, :], LAM[:, 0, 2, :]
    V(lambda e: e.tensor_tensor(out=L0re, in0=mag, in1=cs_, op=ALU.mult))
    V(lambda e: e.tensor_tensor(out=L0im, in0=mag, in1=sn, op=ALU.mult))
    V(lambda e: e.tensor_scalar(out=L0nim, in0=L0im, scalar1=-1.0, scalar2=None, op0=ALU.mult))

    def square_step(k):
        a_re, a_im = LAM[:, k - 1, 0, :], LAM[:, k - 1, 1, :]
        o_re, o_im, o_nim = LAM[:, k, 0, :], LAM[:, k, 1, :], LAM[:, k, 2, :]
        V(lambda e: e.tensor_tensor(out=t1, in0=a_re, in1=a_re, op=ALU.mult))
        V(lambda e: e.tensor_tensor(out=t2, in0=a_im, in1=a_im, op=ALU.mult))
        V(lambda e: e.tensor_tensor(out=o_re, in0=t1, in1=t2, op=ALU.subtract))
        V(lambda e: e.scalar_tensor_tensor(out=o_im, in0=a_re, scalar=2.0, in1=a_im,
                                           op0=ALU.mult, op1=ALU.mult))
        V(lambda e: e.tensor_scalar(out=o_nim, in0=o_im, scalar1=-1.0, scalar2=None, op0=ALU.mult))

    for k in range(1, NKS):
        square_step(k)
    V(lambda e: e.tensor_scalar(out=u_, in0=L0re, scalar1=-1.0, scalar2=None, op0=ALU.add))
    V(lambda e: e.tensor_tensor(out=t1, in0=lre, in1=lre, op=ALU.mult), [prm_t])
    V(lambda e: e.tensor_tensor(out=t2, in0=lim, in1=lim, op=ALU.mult), [prm_t])
    V(lambda e: e.tensor_tensor(out=t1, in0=t1, in1=t2, op=ALU.add))
    V(lambda e: e.reciprocal(out=inv, in_=t1))
    V(lambda e: e.tensor_tensor(out=t1, in0=u_, in1=lre, op=ALU.mult), [prm_t])
    V(lambda e: e.tensor_tensor(out=t2, in0=L0im, in1=lim, op=ALU.mult), [prm_t])
    V(lambda e: e.tensor_tensor(out=t1, in0=t1, in1=t2, op=ALU.add))
    V(lambda e: e.tensor_tensor(out=sn, in0=t1, in1=inv, op=ALU.mult))
    V(lambda e: e.tensor_tensor(out=t1, in0=L0im, in1=lre, op=ALU.mult), [prm_t])
    V(lambda e: e.tensor_tensor(out=t2, in0=u_, in1=lim, op=ALU.mult), [prm_t])
    V(lambda e: e.tensor_tensor(out=t1, in0=t1, in1=t2, op=ALU.subtract))
    V(lambda e: e.tensor_tensor(out=cs_, in0=t1, in1=inv, op=ALU.mult))
    crb = sn.unsqueeze(2).to_broadcast([128, NF, 16])
    cib = cs_.unsqueeze(2).to_broadcast([128, NF, 16])
    Cre, Cim = Cin[:, :, 0, :], Cin[:, :, 1, :]
    V(lambda e: e.tensor_tensor(out=ctmp[0], in0=Cre, in1=crb, op=ALU.mult), [Cin_t])
    V(lambda e: e.tensor_tensor(out=ctmp[1], in0=Cim, in1=cib, op=ALU.mult), [Cin_t])
    V(lambda e: e.tensor_tensor(out=Cp[:, :, 0, :], in0=ctmp[0], in1=ctmp[1], op=ALU.subtract))
    V(lambda e: e.tensor_tensor(out=ctmp[0], in0=Cre, in1=cib, op=ALU.mult), [Cin_t])
    V(lambda e: e.tensor_tensor(out=ctmp[1], in0=Cim, in1=crb, op=ALU.mult), [Cin_t])
    V(lambda e: e.tensor_tensor(out=ctmp[0], in0=ctmp[0], in1=ctmp[1], op=ALU.add))
    V(lambda e: e.tensor_scalar(out=Cp[:, :, 1, :], in0=ctmp[0], scalar1=-1.0, scalar2=None, op0=ALU.mult))

    def place(a, gp):
        j = gp % 4
        c0 = 32 * j + 16 * a
        rows = slice(64 * a, 64 * a + 64)
        P.op("dve", lambda e: e.tensor_copy(out=Cblk[rows, 2 * gp:2 * gp + 2, :, c0:c0 + 16],
                                            in_=Cp[rows, 2 * gp:2 * gp + 2, :, :]),
             [sm_t, Cblk_t], [Cblk_t])

    for a in range(2):
        for gp in range(8):
            place(a, gp)

    def u_blk(tile, blk):
        pu, pu_t = ps[(tile * NB + blk) % 2], ps_t[(tile * NB + blk) % 2]
        proj_feat(P, B, tile * 128, 128, blk, pu, pu_t)
        P.op("act", lambda e: e.copy(out=ubf[:, tile, blk * 512:(blk + 1) * 512], in_=pu[:]), [pu_t], [ubf_t])

    for tile in range(2):
        for blk in range(NB):
            u_blk(tile, blk)

    ks = [[B.hn[:, 2 * (2 * s_ + r) : 2 * (2 * s_ + r) + 2, :].bitcast(F32).rearrange("p a t -> p (a t)")
           for r in range(2)] for s_ in range(2)]
    ks_t = B.hn_t

    def scan_pass(gp, d):
        tile, q, gpl = gp // 4, (gp % 4) // 2, gp % 2
        f = 2 * gp + d
        rows = slice(64 * q, 64 * q + 64)

        def bu(blk):
            sl = slice(blk * 512, (blk + 1) * 512)
            for r in range(2):
                pp, pp_t = ps[2 + r], ps_t[2 + r]
                P.op("pe", lambda e, r=r, pp=pp: e.matmul(pp[:], lhsT=Bb[rows, tile, d, r, gpl, :],
                                                         rhs=ubf[rows, tile, sl], start=True, stop=True),
                     [Bb_t, ubf_t], [pp_t])
                P.op("act", lambda e, r=r, pp=pp: e.copy(out=ks[0][r][:, sl], in_=pp[:]), [pp_t], [ks_t])

        for blk in range(NB):
            bu(blk)

        def ks_round(k):
            s = 1 << k
            src, dst = ks[k % 2], ks[(k + 1) % 2]
            if d == 0:
                hi, lo, hd = slice(s, SEQ), slice(0, SEQ - s), slice(0, s)
            else:
                hi, lo, hd = slice(0, SEQ - s), slice(s, SEQ), slice(SEQ - s, SEQ)
            a_re = LAM[:, k, 0, f:f + 1]
            a_im = LAM[:, k, 1, f:f + 1]
            a_nim = LAM[:, k, 2, f:f + 1]
            stt = lambda e, o, i0, sc, i1: e.scalar_tensor_tensor(out=o, in0=i0, scalar=sc, in1=i1,
                                                                  op0=ALU.mult, op1=ALU.add)
            P.op("dve", lambda e: stt(e, dst[0][:, hi], src[0][:, lo], a_re, src[0][:, hi]), [ks_t, sm_t], [ks_t])
            P.op("dve", lambda e: stt(e, dst[0][:, hi], src[1][:, lo], a_nim, dst[0][:, hi]), [ks_t, sm_t], [ks_t])
            P.op("dve", lambda e: stt(e, dst[1][:, hi], src[1][:, lo], a_re, src[1][:, hi]), [ks_t, sm_t], [ks_t])
            P.op("dve", lambda e: stt(e, dst[1][:, hi], src[0][:, lo], a_im, dst[1][:, hi]), [ks_t, sm_t], [ks_t])
            P.op("act", lambda e: e.copy(out=dst[0][:, hd], in_=src[0][:, hd]), [ks_t], [ks_t])
            P.op("act", lambda e: e.copy(out=dst[1][:, hd], in_=src[1][:, hd]), [ks_t], [ks_t])

        for k in range(NKS):
            ks_round(k)

        def yout(blk):
            sl = slice(blk * 512, (blk + 1) * 512)
            py, py_t = ps[4 + blk % 2], ps_t[4 + blk % 2]
            P.op("pe", lambda e: e.matmul(py[:], lhsT=Cblk[:, f, 0, :], rhs=ks[0][0][:, sl], start=True, stop=False),
                 [Cblk_t, ks_t], [py_t])
            P.op("pe", lambda e: e.matmul(py[:], lhsT=Cblk[:, f, 1, :], rhs=ks[0][1][:, sl], start=False, stop=True),
                 [Cblk_t, ks_t], [py_t])
            P.op("dve", lambda e: e.tensor_tensor(out=Y32[:, tile, sl], in0=Y32[:, tile, sl], in1=py[:], op=ALU.add),
                 [py_t, Y_t], [Y_t])

        for blk in range(NB):
            yout(blk)

    for gp in range(8):
        for d in range(2):
            scan_pass(gp, d)

    def fin(tile, blk):
        sl = slice(blk * 512, (blk + 1) * 512)
        P.op("dve", lambda e: e.scalar_tensor_tensor(out=ytmp, in0=ubf[:, tile, sl], scalar=dsk[:, tile:tile + 1],
                                                     in1=Y32[:, tile, sl], op0=ALU.mult, op1=ALU.add),
             [ubf_t, dsk_t, Y_t], [ytmp_t])
        P.op("act", lambda e: e.activation(out=YS[:, tile, sl], in_=ytmp, func=AF.Gelu_apprx_tanh),
             [ytmp_t], [YS_t])

    for tile in range(2):
        for blk in range(NB):
            fin(tile, blk)
    for tile in range(2):
        for h in range(2):
            P.dma("pool", out_d[tile][h], YS[:, tile, h * TOK:(h + 1) * TOK], [YS_t], [B.yout_t])


def build_B(do_attn=True, do_gla=True, do_s5=True):
    P = Prog()
    hn_d = P.dram("hnT", [KD, 128, SEQ], BF16, "ExternalInput")
    wa_d = P.dram("w_attn", [128, KD, 832], F32, "ExternalInput")
    cos_d = P.dram("rope_cos", [128, SEQ], F32, "ExternalInput")
    sin_d = P.dram("rope_sin", [128, SEQ], F32, "ExternalInput")
    gq_d = P.dram("gq", [128, 2], F32, "ExternalInput")
    gk_d = P.dram("gk", [128, 2], F32, "ExternalInput")
    wg_d = P.dram("w_gla", [128, KD, 1056], F32, "ExternalInput")
    wal_d = P.dram("wal", [32, 2, 256], F32, "ExternalInput")
    nb_d = P.dram("b_alpha", [128, 2, 2], F32, "ExternalInput")
    gn_d = P.dram("gla_gain", [128, 1], F32, "ExternalInput")
    cst_d = P.dram("cst", [128, 3, 128], F32, "ExternalInput")
    ws_d = P.dram("w_s5", [128, KD, 256], F32, "ExternalInput")
    bblk_d = P.dram("s5_bblk", [128, 2, 2, 2, 2, 128], F32, "ExternalInput")
    prm_d = P.dram("s5_prm", [128, 16, 3], F32, "ExternalInput")
    c_d = P.dram("s5_c", [128, 16, 2, 16], F32, "ExternalInput")
    dsk_d = P.dram("s5_d", [128, 2], F32, "ExternalInput")
    ya_d = P.dram("y_attn", [2, 128, SEQ], BF16, "ExternalOutput")
    yg_d = P.dram("y_gla", [2, 128, SEQ], BF16, "ExternalOutput")
    ys_d = P.dram("y_s5", [2, 128, SEQ], BF16, "ExternalOutput")
    B = setup_B(P)
    for k in range(KD):
        P.dma("sp", B.hn[:, k, :], hn_d[k], [], [B.hn_t])
    halves = lambda d: [[d[t][:, h * TOK:(h + 1) * TOK] for h in range(2)] for t in range(2)]
    ya_d, yg_d, ys_d = halves(ya_d), halves(yg_d), halves(ys_d)
    if do_attn:
        attn_branch(P, B, wa_d, cos_d, sin_d, gq_d, gk_d, ya_d)
        P.barrier()
    if do_gla:
        gla_branch(P, B, wg_d, wal_d, nb_d, gn_d, cst_d, yg_d)
        P.barrier()
    if do_s5:
        s5_branch(P, B, ws_d, bblk_d, prm_d, c_d, dsk_d, ys_d)
    P.barrier()
    return P.emit()


S5_W, GLA_W, ATT_W, ATT_KV = 512, 512, 512, 128
OFF_S5 = 0
OFF_GQ, OFF_GK, OFF_GV, OFF_GG = 512, 1024, 1536, 2048
OFF_ZF, OFF_ZB = 2560, 2576
OFF_AQ, OFF_AK, OFF_AV = 2592, 3104, 3232


def w_tile(w):
    return np.ascontiguousarray(w.reshape(KD, 128, w.shape[1]).transpose(1, 0, 2))


def rope_partner(n_heads):
    idx = np.arange(64)
    within = idx % 32
    partner = np.where(within < 16, idx + 16, idx - 16)
    return np.concatenate([h * 64 + partner for h in range(n_heads)])


def rope_tables():
    half = 16
    inv_freq = (10000.0 ** (-np.arange(half, dtype=np.float32) * 2.0 / 32)).astype(np.float32)
    t = np.arange(SEQ)
    rows = (t // 64).astype(np.float32)
    cols = (t % 64).astype(np.float32)
    ang = np.zeros((64, SEQ), np.float32)
    sign = np.zeros((64, 1), np.float32)
    for i in range(64):
        pos = rows if i < 32 else cols
        ang[i] = pos * inv_freq[i % 16]
        sign[i] = -1.0 if (i % 32) < 16 else 1.0
    cos = np.cos(ang).astype(np.float32)
    sin = (np.sin(ang) * sign).astype(np.float32)
    return np.concatenate([cos, cos], 0), np.concatenate([sin, sin], 0)


def gla_consts():
    j = np.arange(128)[:, None]
    i = np.arange(128)[None, :]
    c = np.zeros((128, 3, 128), np.float32)
    c[:, 0, :] = (j <= i)
    c[:, 1, :] = (j >= i)
    c[:, 2, :] = (j == i)
    return c


def maps_B(inp, li, hn_full):
    w_in = inp["w_in"][li]
    cosT, sinT = rope_tables()
    cst = gla_consts()
    maps = []
    for c in range(NCORES):
        b, k = c // 2, c % 2
        m = {"rope_cos": cosT, "rope_sin": sinT, "cst": cst}
        if hn_full[b] is not None:
            m["hnT"] = hn_full[b]
        q = w_in[:, OFF_AQ + 256 * k: OFF_AQ + 256 * (k + 1)]
        kk = w_in[:, OFF_AK + 64 * k: OFF_AK + 64 * (k + 1)]
        vv = w_in[:, OFF_AV + 64 * k: OFF_AV + 64 * (k + 1)]
        qrot = q[:, rope_partner(4)]
        krot = kk[:, rope_partner(1)]
        wa = np.concatenate([q, qrot, kk, kk, krot, krot, vv], axis=1)
        m["w_attn"] = w_tile(wa)
        gq = inp["attn_q_norm"][li]
        gk = inp["attn_k_norm"][li]
        pr = rope_partner(1)
        m["gq"] = np.ascontiguousarray(np.stack([np.tile(gq, 2), np.tile(gq[pr], 2)], 1))
        m["gk"] = np.ascontiguousarray(np.stack([np.tile(gk, 2), np.tile(gk[pr], 2)], 1))
        sl = slice(256 * k, 256 * (k + 1))
        wg = np.concatenate([w_in[:, OFF_GQ:OFF_GK][:, sl], w_in[:, OFF_GK:OFF_GV][:, sl],
                             w_in[:, OFF_GV:OFF_GG][:, sl], w_in[:, OFF_GG:OFF_ZF][:, sl],
                             w_in[:, OFF_ZF:OFF_AQ]], axis=1)
        m["w_gla"] = w_tile(wg)
        wal = np.zeros((32, 2, 256), np.float32)
        wal[0:16, 0, :] = inp["gla_w_alpha"][li, 0][:, sl]
        wal[16:32, 1, :] = inp["gla_w_alpha"][li, 1][:, sl]
        m["wal"] = wal
        ba = inp["gla_b_alpha"][li][:, sl].reshape(2, 2, 128)
        m["b_alpha"] = np.ascontiguousarray(ba.transpose(2, 0, 1))
        m["gla_gain"] = np.ascontiguousarray(inp["gla_norm"][li].reshape(128, 1))
        m["w_s5"] = w_tile(w_in[:, OFF_S5 + 256 * k: OFF_S5 + 256 * (k + 1)])
        g0 = 16 * k
        bb = np.zeros((128, 2, 2, 2, 2, 128), np.float32)
        braw = [inp["s5_b_re"][li], inp["s5_b_im"][li]]
        for tile in range(2):
            for p in range(128):
                g = g0 + tile * 8 + p // 16
                h = p % 16
                gpl = (p % 64) // 32
                a = (p % 32) // 16
                for d in range(2):
                    for r in range(2):
                        bb[p, tile, d, r, gpl, a * 64:(a + 1) * 64] = braw[r][d, g, :, h]
        m["s5_bblk"] = bb
        prm = np.zeros((128, 16, 3), np.float32)
        cc = np.zeros((128, 16, 2, 16), np.float32)
        craw = [inp["s5_c_re"][li], inp["s5_c_im"][li]]
        for gp in range(8):
            for a in range(2):
                g = g0 + 2 * gp + a
                for d in range(2):
                    f = 2 * gp + d
                    prm[a * 64:(a + 1) * 64, f, 0] = inp["s5_lambda_re"][li, d, g]
                    prm[a * 64:(a + 1) * 64, f, 1] = inp["s5_lambda_im"][li, d, g]
                    prm[a * 64:(a + 1) * 64, f, 2] = inp["s5_log_dt"][li, d, g]
                    for r in range(2):
                        cc[a * 64:(a + 1) * 64, f, r, :] = craw[r][d, g].T
        m["s5_prm"] = prm
        m["s5_c"] = cc
        m["s5_d"] = np.ascontiguousarray(inp["s5_d"][li][256 * k:256 * (k + 1)].reshape(2, 128).T)
        maps.append(m)
    return maps


def merge_phase(P, C, ys_d, yg_d, ya_d, wglu_d, wbr_d, wgate_d, bgate_d, wout_d, y_all=None, sel_d=None):
    half = C.ntok // 2
    bg, bg_t = load_vec(P, C, "bgate_sb", bgate_d, 24)
    if y_all is not None:
        sel, sel_t = load_vec(P, C, "sel_sb", sel_d, 2)
        ytmp = [P.sbuf(f"ytmp{i}", [128, half], BF16) for i in range(2)]
        ytmp_t = [P.tok() for i in range(2)]
    for tt in range(2):
        P.barrier()
        A = C.arena
        o = 0

        def take(nwords, shape_c):
            nonlocal o
            ap = A[:, o:o + nwords].bitcast(BF16).rearrange("p (c t) -> p c t", c=shape_c)
            o += nwords
            return ap

        yb = [take(4 * half // 2, 4) for _ in range(3)]
        ysg = take(4 * half // 2, 4)
        mg = take(8 * half // 2, 8)
        yb_t = [P.tok() for _ in range(3)]
        ysg_t, mg_t = P.tok(), P.tok()
        t0 = tt * half
        def pick(br, gt):
            for j in range(2):
                r_ = gt // 2
                P.dma("sp", ytmp[j], y_all[br][gt % 2][j][r_ * 128:(r_ + 1) * 128, t0:t0 + half],
                      [C.yall_t], [ytmp_t[j]])
            P.op("dve", lambda e: e.tensor_scalar(out=ytmp[0], in0=ytmp[0], scalar1=sel[:, 0:1], scalar2=None,
                                                  op0=ALU.mult), [ytmp_t[0], sel_t], [ytmp_t[0]])
            P.op("dve", lambda e: e.scalar_tensor_tensor(out=yb[br][:, gt, :], in0=ytmp[1], scalar=sel[:, 1:2],
                                                         in1=ytmp[0], op0=ALU.mult, op1=ALU.add),
                 [ytmp_t[0], ytmp_t[1], sel_t], [yb_t[br]])

        for br, src in enumerate((ys_d, yg_d, ya_d)):
            for k in range(4):
                if y_all is None:
                    P.dma("sp", yb[br][:, k, :], src[k, :, t0:t0 + half], [], [yb_t[br]])
                else:
                    pick(br, k)
        nb2 = half // 512

        def wload(src_ap, nchunk, slot_cols):
            i = C.wcnt % 2
            C.wcnt += 1
            st = C.wst[i].rearrange("p a k f -> p (a k) f")
            wb = C.wb[i].rearrange("p a k f -> p (a k) f")
            P.dma("sp", st[:, slot_cols:slot_cols + nchunk, :], src_ap, [], [C.wst_t[i]])
            P.op("pool", lambda e: e.tensor_copy(out=wb[:, slot_cols:slot_cols + nchunk, :],
                                                 in_=st[:, slot_cols:slot_cols + nchunk, :]),
                 [C.wst_t[i]], [C.wb_t[i]])
            return wb, C.wb_t[i]

        def glu_m(m):
            wb, wb_t = wload(wglu_d[m], 4, 0)
            for b in range(nb2):
                sl = slice(b * 512, (b + 1) * 512)
                j = C.gcnt % 2
                C.gcnt += 1
                for k in range(4):
                    P.op("pe", lambda e, k=k, j=j, sl=sl: e.matmul(C.pg[j][:], lhsT=wb[:, k, :], rhs=yb[0][:, k, sl],
                                                                 start=(k == 0), stop=(k == 3)),
                         [wb_t, yb_t[0]], [C.pg_t[j]])
                P.op("act", lambda e, j=j: e.activation(out=C.sg[j][:], in_=C.pg[j][:], func=AF.Sigmoid),
                     [C.pg_t[j]], [C.sg_t[j]])
                P.op("dve", lambda e, j=j, sl=sl: e.tensor_tensor(out=ysg[:, m, sl], in0=C.sg[j][:],
                                                                 in1=yb[0][:, m, sl], op=ALU.mult),
                     [C.sg_t[j], yb_t[0]], [ysg_t])

        for m in range(4):
            glu_m(m)

        def merge_m(m):
            for br in range(3):
                ysrc, ysrc_t = (ysg, ysg_t) if br == 0 else (yb[br], yb_t[br])
                i = C.wcnt % 2
                wb, wb_t = wload(wbr_d[br, m], 4, 0)
                st = C.wst[i].rearrange("p a k f -> p (a k) f")
                P.dma("sp", st[:, 4:12, :], wgate_d[br * 8 + m], [], [C.wst_t[i]])
                P.op("pool", lambda e, st=st, wb=wb: e.tensor_copy(out=wb[:, 4:12, :], in_=st[:, 4:12, :]),
                     [C.wst_t[i]], [wb_t])
                for b in range(nb2):
                    sl = slice(b * 512, (b + 1) * 512)
                    tsl = slice(t0 + b * 512, t0 + (b + 1) * 512)
                    j = C.gcnt % 2
                    C.gcnt += 1
                    for k in range(4):
                        P.op("pe", lambda e, k=k, j=j, sl=sl, wb=wb, ysrc=ysrc: e.matmul(
                            C.pg[j][:], lhsT=wb[:, k, :], rhs=ysrc[:, k, sl], start=(k == 0), stop=(k == 3)),
                            [wb_t, ysrc_t], [C.pg_t[j]])
                    for k in range(KD):
                        P.op("pe", lambda e, k=k, j=j, tsl=tsl, wb=wb: e.matmul(
                            C.pu[j][:], lhsT=wb[:, 4 + k, :], rhs=C.hn[:, k, tsl], start=(k == 0), stop=(k == KD - 1)),
                            [wb_t] + C.hnt, [C.pu_t[j]])
                    P.op("act", lambda e, j=j, br=br: e.activation(
                        out=C.sg[j][:], in_=C.pu[j][:], func=AF.Sigmoid, bias=bg[:, br * 8 + m:br * 8 + m + 1]),
                        [C.pu_t[j], bg_t], [C.sg_t[j]])
                    if br == 0:
                        P.op("dve", lambda e, j=j, b=b: e.tensor_tensor(
                            out=C.macc[b][:], in0=C.sg[j][:], in1=C.pg[j][:], op=ALU.mult),
                            [C.sg_t[j], C.pg_t[j]], [C.macc_t[b]])
                    else:
                        P.op("dve", lambda e, j=j: e.tensor_tensor(
                            out=C.sg[j][:], in0=C.sg[j][:], in1=C.pg[j][:], op=ALU.mult),
                            [C.sg_t[j], C.pg_t[j]], [C.sg_t[j]])
                        if br == 1:
                            P.op("dve", lambda e, j=j, b=b: e.tensor_tensor(
                                out=C.macc[b][:], in0=C.macc[b][:], in1=C.sg[j][:], op=ALU.add),
                                [C.sg_t[j], C.macc_t[b]], [C.macc_t[b]])
                        else:
                            P.op("dve", lambda e, j=j, b=b, sl=sl: e.tensor_tensor(
                                out=mg[:, m, sl], in0=C.macc[b][:], in1=C.sg[j][:], op=ALU.add),
                                [C.sg_t[j], C.macc_t[b]], [mg_t])

        for m in range(KD):
            merge_m(m)

        def out_m(m):
            wb, wb_t = wload(wout_d[m], 8, 0)
            for b in range(nb2):
                sl = slice(b * 512, (b + 1) * 512)
                tsl = slice(t0 + b * 512, t0 + (b + 1) * 512)
                gb = (t0 + b * 512) // 512
                j = C.ocnt % 2
                C.ocnt += 1
                for k in range(KD):
                    P.op("pe", lambda e, k=k, j=j, sl=sl: e.matmul(C.po[j][:], lhsT=wb[:, k, :], rhs=mg[:, k, sl],
                                                                 start=(k == 0), stop=(k == KD - 1)),
                         [wb_t, mg_t], [C.po_t[j]])
                P.op("dve", lambda e, j=j, tsl=tsl: e.tensor_tensor(out=C.x[:, m, tsl], in0=C.x[:, m, tsl],
                                                                    in1=C.po[j][:], op=ALU.add),
                     [C.po_t[j], C.xt[gb]], [C.xt[gb]])

        for m in range(KD):
            out_m(m)
    P.barrier()


def build_C(last):
    P = Prog()
    x_d = P.dram("xT", [KD, 128, TOK], F32, "ExternalInput")
    hn_d = P.dram("hnT", [KD, 128, TOK], BF16, "ExternalInput")
    ys_d = P.dram("ys5", [4, 128, TOK], BF16, "ExternalInput")
    yg_d = P.dram("ygla", [4, 128, TOK], BF16, "ExternalInput")
    ya_d = P.dram("yattn", [4, 128, TOK], BF16, "ExternalInput")
    wglu_d = P.dram("wglu", [4, 128, 4, 128], F32, "ExternalInput")
    wbr_d = P.dram("wbr", [3, KD, 128, 4, 128], F32, "ExternalInput")
    wgate_d = P.dram("wgate", [24, 128, KD, 128], F32, "ExternalInput")
    bgate_d = P.dram("bgate", [128, 24], F32, "ExternalInput")
    wout_d = P.dram("wout", [KD, 128, KD, 128], F32, "ExternalInput")
    g2_d = P.dram("g_ffn2", [128, KD], F32, "ExternalInput")
    wg2_d = P.dram("wg2", [NFF, 128, KD, 128], F32, "ExternalInput")
    wu2_d = P.dram("wu2", [NFF, 128, KD, 128], F32, "ExternalInput")
    wd2_d = P.dram("wd2", [KD, 128, NFF, 128], F32, "ExternalInput")
    g3_d = P.dram("g_next", [128, KD], F32, "ExternalInput")
    if not last:
        wg1_d = P.dram("wg1", [NFF, 128, KD, 128], F32, "ExternalInput")
        wu1_d = P.dram("wu1", [NFF, 128, KD, 128], F32, "ExternalInput")
        wd1_d = P.dram("wd1", [KD, 128, NFF, 128], F32, "ExternalInput")
        g4_d = P.dram("g_mix", [128, KD], F32, "ExternalInput")
        xo_d = P.dram("xo", [KD, 128, TOK], F32, "ExternalOutput")
        hno_d = P.dram("hno", [KD, 128, TOK], BF16, "ExternalOutput")
    else:
        out_d = P.dram("outT", [KD, 128, TOK], F32, "ExternalOutput")
    C = setup_common(P)
    eps_const(P, C)
    C.macc = [P.sbuf(f"macc{i}", [128, 512], F32) for i in range(2)]
    C.macc_t = [P.tok() for i in range(2)]
    g2, g2t = load_vec(P, C, "g2", g2_d, KD)
    g3, g3t = load_vec(P, C, "g3", g3_d, KD)
    load_x(P, C, x_d)
    for k in range(KD):
        for b in range(C.nblk):
            sl = slice(b * 512, (b + 1) * 512)
            P.dma("sp", C.hn[:, k, sl], hn_d[k, :, sl], [], [C.hnt[b]])
    ffn_alloc(P, C)
    merge_phase(P, C, ys_d, yg_d, ya_d, wglu_d, wbr_d, wgate_d, bgate_d, wout_d)
    for b in range(C.nblk):
        rmsnorm_block(P, C, b, g2, g2t, C.hn, C.hnt)
    ffn(P, C, wg2_d, wu2_d, wd2_d)
    if not last:
        g4, g4t = load_vec(P, C, "g4", g4_d, KD)
        for b in range(C.nblk):
            rmsnorm_block(P, C, b, g3, g3t, C.hn, C.hnt)
        ffn(P, C, wg1_d, wu1_d, wd1_d)
        for b in range(C.nblk):
            rmsnorm_block(P, C, b, g4, g4t, C.hn, C.hnt)
        store_feat(P, C, C.x, C.xt, xo_d)
        store_feat(P, C, C.hn, C.hnt, hno_d)
    else:
        for b in range(C.nblk):
            rmsnorm_block(P, C, b, g3, g3t, C.x, C.xt)
        store_feat(P, C, C.x, C.xt, out_d)
    P.barrier()
    return P.emit()


def _run(nc, maps):
    res = run_bass_kernel_spmd(nc, maps, core_ids=list(range(NCORES)))
    return res.results


def _ffn_maps(inp, pre, li, names):
    return {names[0]: tile_w_kc(inp[pre + "_w_gate"][li]), names[1]: tile_w_kc(inp[pre + "_w_up"][li]),
            names[2]: tile_w_kc(inp[pre + "_w_down"][li])}


def kernel_unfused(**inputs):
    inp = {k: np.asarray(v) for k, v in inputs.items()}
    x = inp["x"].astype(np.float32)
    base = _ffn_maps(inp, "ffn1", 0, ("wg", "wu", "wd"))
    base["g_ffn"] = vec128(inp["ffn1_norm"][0])
    base["g_mix"] = vec128(inp["mix_norm"][0])
    maps = []
    for c in range(NCORES):
        b, k = c // 2, c % 2
        m = dict(base)
        m["xT"] = feat_major(x[b, k * TOK:(k + 1) * TOK])
        maps.append(m)
    res = _run(build_A(), maps)
    xT = [r["xo"] for r in res]
    hnT = [r["hno"] for r in res]
    ncB = build_B()
    out = np.zeros((BATCH, SEQ, D_MODEL), np.float32)
    for li in range(DEPTH):
        last = li == DEPTH - 1
        hn_full = [np.ascontiguousarray(np.concatenate([hnT[2 * b], hnT[2 * b + 1]], axis=2)) for b in range(BATCH)]
        resB = _run(ncB if li == 0 else build_B(), maps_B(inp, li, hn_full))
        base = _ffn_maps(inp, "ffn2", li, ("wg2", "wu2", "wd2"))
        base["g_ffn2"] = vec128(inp["ffn2_norm"][li])
        base["wglu"] = tile_w_kc(inp["s5_w_glu"][li])
        base["wbr"] = np.stack([tile_w_kc(inp["w_branch_s5"][li]), tile_w_kc(inp["w_branch_gla"][li]),
                                tile_w_kc(inp["w_branch_attn"][li])], 0)
        base["wgate"] = tile_w_kc(inp["w_merge_gate"][li])
        base["bgate"] = vec128(inp["b_merge_gate"][li])
        base["wout"] = tile_w_kc(inp["w_out"][li])
        if not last:
            base.update(_ffn_maps(inp, "ffn1", li + 1, ("wg1", "wu1", "wd1")))
            base["g_next"] = vec128(inp["ffn1_norm"][li + 1])
            base["g_mix"] = vec128(inp["mix_norm"][li + 1])
        else:
            base["g_next"] = vec128(inp["final_norm"])
        maps = []
        for c in range(NCORES):
            b, k = c // 2, c % 2
            ts = slice(k * TOK, (k + 1) * TOK)
            m = dict(base)
            m["xT"] = xT[c]
            m["hnT"] = hnT[c]
            for key, okey in (("ys5", "y_s5"), ("ygla", "y_gla"), ("yattn", "y_attn")):
                full = np.concatenate([resB[2 * b][okey], resB[2 * b + 1][okey]], axis=0)
                m[key] = np.ascontiguousarray(full[:, :, ts])
            maps.append(m)
        res = _run(build_C(last), maps)
        if not last:
            xT = [r["xo"] for r in res]
            hnT = [r["hno"] for r in res]
        else:
            for c in range(NCORES):
                b, k = c // 2, c % 2
                out[b, k * TOK:(k + 1) * TOK, :] = res[c]["outT"].reshape(D_MODEL, TOK).T
    return out


MEGA_WORDS = 53000


def _ffn_decl(P, pre):
    return (P.dram(pre + "_wg", [NFF, 128, KD, 128], F32, "ExternalInput"),
            P.dram(pre + "_wu", [NFF, 128, KD, 128], F32, "ExternalInput"),
            P.dram(pre + "_wd", [KD, 128, NFF, 128], F32, "ExternalInput"))


def build_fused():
    P = Prog()
    P.enable_mega(MEGA_WORDS)
    x_d = P.dram("xT", [KD, 128, TOK], F32, "ExternalInput")
    sel_d = P.dram("sel", [128, 2], F32, "ExternalInput")
    cos_d = P.dram("rope_cos", [128, SEQ], F32, "ExternalInput")
    sin_d = P.dram("rope_sin", [128, SEQ], F32, "ExternalInput")
    cst_d = P.dram("cst", [128, 3, 128], F32, "ExternalInput")
    gfin_d = P.dram("g_final", [128, KD], F32, "ExternalInput")
    L = []
    for li in range(DEPTH):
        d = Ctx()
        p = f"l{li}_"
        d.ffn1 = _ffn_decl(P, p + "ffn1")
        d.ffn2 = _ffn_decl(P, p + "ffn2")
        d.g_ffn1 = P.dram(p + "g_ffn1", [128, KD], F32, "ExternalInput")
        d.g_mix = P.dram(p + "g_mix", [128, KD], F32, "ExternalInput")
        d.g_ffn2 = P.dram(p + "g_ffn2", [128, KD], F32, "ExternalInput")
        d.wa = P.dram(p + "w_attn", [128, KD, 832], F32, "ExternalInput")
        d.gq = P.dram(p + "gq", [128, 2], F32, "ExternalInput")
        d.gk = P.dram(p + "gk", [128, 2], F32, "ExternalInput")
        d.wg = P.dram(p + "w_gla", [128, KD, 1056], F32, "ExternalInput")
        d.wal = P.dram(p + "wal", [32, 2, 256], F32, "ExternalInput")
        d.nb = P.dram(p + "b_alpha", [128, 2, 2], F32, "ExternalInput")
        d.gn = P.dram(p + "gla_gain", [128, 1], F32, "ExternalInput")
        d.ws = P.dram(p + "w_s5", [128, KD, 256], F32, "ExternalInput")
        d.bblk = P.dram(p + "s5_bblk", [128, 2, 2, 2, 2, 128], F32, "ExternalInput")
        d.prm = P.dram(p + "s5_prm", [128, 16, 3], F32, "ExternalInput")
        d.c = P.dram(p + "s5_c", [128, 16, 2, 16], F32, "ExternalInput")
        d.dsk = P.dram(p + "s5_d", [128, 2], F32, "ExternalInput")
        d.wglu = P.dram(p + "wglu", [4, 128, 4, 128], F32, "ExternalInput")
        d.wbr = P.dram(p + "wbr", [3, KD, 128, 4, 128], F32, "ExternalInput")
        d.wgate = P.dram(p + "wgate", [24, 128, KD, 128], F32, "ExternalInput")
        d.bgate = P.dram(p + "bgate", [128, 24], F32, "ExternalInput")
        d.wout = P.dram(p + "wout", [KD, 128, KD, 128], F32, "ExternalInput")
        L.append(d)
    out_d = P.dram("outT", [KD, 128, TOK], F32, "ExternalOutput")
    for li in range(DEPTH):
        d = L[li]
        p = f"l{li}_"
        d.hn_src = [P.dram(p + f"hn_src{k}", [128, TOK], BF16, "Internal") for k in range(KD)]
        d.hn_all = [P.dram(p + f"hn_all{k}", [256, TOK], BF16, "Internal") for k in range(KD)]
        d.y_src = [[[P.dram(p + f"y_src{b}{t}{h}", [128, TOK], BF16, "Internal") for h in range(2)]
                    for t in range(2)] for b in range(3)]
        d.y_all = [[[P.dram(p + f"y_all{b}{t}{h}", [256, TOK], BF16, "Internal") for h in range(2)]
                    for t in range(2)] for b in range(3)]
        d.x_sp = P.dram(p + "x_sp", [KD, 128, TOK], F32, "Internal")

    def park(C, d):
        xs_t = store_feat(P, C, C.x, C.xt, d.x_sp)
        all_t = P.tok("hn_all")
        for k in range(KD):
            t = P.tok()
            P.dma("pool", d.hn_src[k], C.hn[:, k, :], list(C.hnt), [t])
            P.collective(d.hn_src[k], d.hn_all[k], [t], [all_t])
        return xs_t, all_t

    C = setup_common(P)
    eps_const(P, C)
    g1, g1t = load_vec(P, C, "g1", L[0].g_ffn1, KD)
    g2, g2t = load_vec(P, C, "g2", L[0].g_mix, KD)
    load_x(P, C, x_d)
    for b in range(C.nblk):
        rmsnorm_block(P, C, b, g1, g1t, C.hn, C.hnt)
    ffn(P, C, *L[0].ffn1)
    for b in range(C.nblk):
        rmsnorm_block(P, C, b, g2, g2t, C.hn, C.hnt)
    xs_t, hn_all_t = park(C, L[0])

    for li in range(DEPTH):
        d = L[li]
        last = li == DEPTH - 1
        P.barrier()
        P.phase_reset()
        B = setup_B(P)
        for r in range(2):
            for k in range(KD):
                P.dma("sp", B.hn[:, k, r * TOK:(r + 1) * TOK], d.hn_all[k][r * 128:(r + 1) * 128, :],
                      [hn_all_t], [B.hn_t])
        yv = d.y_src
        attn_branch(P, B, d.wa, cos_d, sin_d, d.gq, d.gk, yv[2])
        P.barrier()
        gla_branch(P, B, d.wg, d.wal, d.nb, d.gn, cst_d, yv[1])
        P.barrier()
        s5_branch(P, B, d.ws, d.bblk, d.prm, d.c, d.dsk, yv[0])
        P.barrier()
        yall_t = P.tok("y_all")
        for b_ in range(3):
            for t_ in range(2):
                for h_ in range(2):
                    P.collective(d.y_src[b_][t_][h_], d.y_all[b_][t_][h_], [], [yall_t])
        P.barrier()
        P.phase_reset()
        C = setup_common(P)
        C.yall_t = yall_t
        eps_const(P, C)
        C.macc = [P.sbuf(f"macc{i}", [128, 512], F32) for i in range(2)]
        C.macc_t = [P.tok() for i in range(2)]
        gA, gAt = load_vec(P, C, "gA", d.g_ffn2, KD)
        for k in range(KD):
            for b in range(C.nblk):
                sl = slice(b * 512, (b + 1) * 512)
                P.dma("sp", C.x[:, k, sl], d.x_sp[k, :, sl], list(xs_t), [C.xt[b]])
                P.dma("sp", C.hn[:, k, sl], d.hn_src[k][:, sl], [], [C.hnt[b]])
        ffn_alloc(P, C)
        merge_phase(P, C, None, None, None, d.wglu, d.wbr, d.wgate, d.bgate, d.wout,
                    y_all=d.y_all, sel_d=sel_d)
        for b in range(C.nblk):
            rmsnorm_block(P, C, b, gA, gAt, C.hn, C.hnt)
        ffn(P, C, *d.ffn2)
        if not last:
            n = L[li + 1]
            gB, gBt = load_vec(P, C, "gB", n.g_ffn1, KD)
            gC, gCt = load_vec(P, C, "gC", n.g_mix, KD)
            for b in range(C.nblk):
                rmsnorm_block(P, C, b, gB, gBt, C.hn, C.hnt)
            ffn(P, C, *n.ffn1)
            for b in range(C.nblk):
                rmsnorm_block(P, C, b, gC, gCt, C.hn, C.hnt)
            xs_t, hn_all_t = park(C, n)
        else:
            gF, gFt = load_vec(P, C, "gF", gfin_d, KD)
            for b in range(C.nblk):
                rmsnorm_block(P, C, b, gF, gFt, C.x, C.xt)
            store_feat(P, C, C.x, C.xt, out_d)
    P.barrier()
    return P.emit()


def maps_fused(inp):
    x = inp["x"].astype(np.float32)
    cosT, sinT = rope_tables()
    common = {"rope_cos": cosT, "rope_sin": sinT, "cst": gla_consts(), "g_final": vec128(inp["final_norm"])}
    per_layer_B = []
    dummy_hn = [None] * BATCH
    for li in range(DEPTH):
        p = f"l{li}_"
        for pre in ("ffn1", "ffn2"):
            common[p + pre + "_wg"] = tile_w_kc(inp[pre + "_w_gate"][li])
            common[p + pre + "_wu"] = tile_w_kc(inp[pre + "_w_up"][li])
            common[p + pre + "_wd"] = tile_w_kc(inp[pre + "_w_down"][li])
        common[p + "g_ffn1"] = vec128(inp["ffn1_norm"][li])
        common[p + "g_mix"] = vec128(inp["mix_norm"][li])
        common[p + "g_ffn2"] = vec128(inp["ffn2_norm"][li])
        common[p + "wglu"] = tile_w_kc(inp["s5_w_glu"][li])
        common[p + "wbr"] = np.stack([tile_w_kc(inp["w_branch_s5"][li]), tile_w_kc(inp["w_branch_gla"][li]),
                                      tile_w_kc(inp["w_branch_attn"][li])], 0)
        common[p + "wgate"] = tile_w_kc(inp["w_merge_gate"][li])
        common[p + "bgate"] = vec128(inp["b_merge_gate"][li])
        common[p + "wout"] = tile_w_kc(inp["w_out"][li])
        per_layer_B.append(maps_B(inp, li, dummy_hn))
    maps = []
    for c in range(NCORES):
        b, k = c // 2, c % 2
        m = dict(common)
        m["xT"] = feat_major(x[b, k * TOK:(k + 1) * TOK])
        sel = np.zeros((128, 2), np.float32)
        sel[:, k] = 1.0
        m["sel"] = sel
        for li in range(DEPTH):
            mb = per_layer_B[li][c]
            for key in ("w_attn", "gq", "gk", "w_gla", "wal", "b_alpha", "gla_gain", "w_s5", "s5_bblk", "s5_prm",
                        "s5_c", "s5_d"):
                m[f"l{li}_" + key] = mb[key]
        maps.append(m)
    return maps


def kernel_fused(**inputs):
    inp = {k: np.asarray(v) for k, v in inputs.items()}
    res = _run(build_fused(), maps_fused(inp))
    out = np.zeros((BATCH, SEQ, D_MODEL), np.float32)
    for c in range(NCORES):
        b, k = c // 2, c % 2
        out[b, k * TOK:(k + 1) * TOK, :] = res[c]["outT"].reshape(D_MODEL, TOK).T
    return out


def kernel(**inputs):
    return kernel_fused(**inputs)
```

```python
import contextlib
import numpy as np
import ml_dtypes
import concourse.bass as bass
import concourse.mybir as mybir
from concourse.bass_utils import run_bass_kernel_spmd

F32 = mybir.dt.float32
BF16 = mybir.dt.bfloat16
AF = mybir.ActivationFunctionType
ALU = mybir.AluOpType
NPBF = ml_dtypes.bfloat16

D_MODEL = 1024
BATCH = 4
SEQ = 4096
DEPTH = 2
D_FF = 2816
EPS = 1e-6
NCORES = 8
TOK = 2048
KD = D_MODEL // 128
NFF = D_FF // 128

ENGINES = ("pe", "act", "dve", "pool", "sp")
import os
PROG_LIMIT = int(os.environ.get("PROG_LIMIT", "100000000"))
N_DMA_SEMS = 24


class Tok:
    __slots__ = ("name", "last_w", "readers", "excl")

    def __init__(self, name, excl=False):
        self.name = name
        self.last_w = None
        self.readers = []
        self.excl = excl


class Op:
    __slots__ = ("eng", "fn", "idx", "eidx", "is_dma", "waits", "signal", "clock", "dma_slot",
                 "dma_val", "dma_known", "prewait", "waited", "cc")


class Prog:
    def __init__(self):
        self.nc = bass.Bass("TRN2", target_bir_lowering=False)
        self.ops = []
        self.eng_ops = {e: [] for e in ENGINES}
        self.known = {e: {f: -1 for f in ENGINES} for e in ENGINES}
        self.known_dma = {e: set() for e in ENGINES}
        self.stack = contextlib.ExitStack()
        self.dma_count = 0
        self.dma_slot_last = [None] * N_DMA_SEMS
        self.ntok = 0
        self.drams = {}
        self.mega = None
        self.moff = 0
        self.banks = None
        self.ncc = 0

    def enable_mega(self, words):
        self.mega = self.stack.enter_context(self.nc.sbuf_tensor("mega", [128, words], F32))
        self.mega_words = words

    def phase_reset(self):
        self.moff = 0

    def sbuf(self, name, shape, dtype=F32):
        if self.mega is None:
            return self.stack.enter_context(self.nc.sbuf_tensor(name, list(shape), dtype))
        shape = list(shape)
        n = 1
        for d in shape[1:]:
            n *= d
        isz = 2 if dtype == BF16 else 4
        words = (n * isz + 3) // 4
        words = (words + 15) // 16 * 16
        assert self.moff + words <= self.mega_words, (name, self.moff, words)
        ap = self.mega[0:shape[0], self.moff:self.moff + words]
        self.moff += words
        if dtype != F32:
            ap = ap.bitcast(dtype)
        ap = ap[:, 0:n]
        if len(shape) > 2:
            names = " ".join(f"d{i}" for i in range(len(shape) - 1))
            kw = {f"d{i}": shape[i + 1] for i in range(len(shape) - 1)}
            ap = ap.rearrange(f"p ({names}) -> p {names}", **kw)
        return ap

    def psum(self, name, shape, dtype=F32):
        return self.stack.enter_context(self.nc.psum_tensor(name, list(shape), dtype))

    def psum_banks(self):
        if self.banks is None:
            bs = [self.psum(f"bank{i}", [128, 512], F32) for i in range(8)]
            ts = [self.tok(f"bank{i}", True) for i in range(8)]
            self.banks = (bs, ts)
        return self.banks

    def collective(self, src_ap, dst_ap, reads, writes):
        def fn(e):
            return e.collective_compute("AllGather", ALU.bypass, replica_groups=[[0, 1], [2, 3], [4, 5], [6, 7]],
                                        ins=[src_ap], outs=[dst_ap])
        return self._add("pool", fn, list(reads), list(writes), True, True)

    def dram(self, name, shape, dtype, kind):
        t = self.nc.dram_tensor(name, list(shape), dtype, kind=kind)
        self.drams[name] = t
        return t.ap()

    def tok(self, name=None, excl=False):
        self.ntok += 1
        return Tok(name or f"t{self.ntok}", excl)

    def _add(self, eng, fn, reads, writes, is_dma, is_cc=False):
        if len(self.ops) >= PROG_LIMIT:
            return None
        op = Op()
        op.cc = None
        op.eng = eng
        op.fn = fn
        op.idx = len(self.ops)
        op.eidx = len(self.eng_ops[eng])
        op.is_dma = is_dma
        op.signal = False
        op.waits = []
        op.prewait = None
        op.waited = False
        known = self.known[eng]
        kd = self.known_dma[eng]
        deps = set()
        for t in reads:
            if t.last_w is not None:
                deps.add(t.last_w)
        for t in writes:
            if t.last_w is not None:
                deps.add(t.last_w)
            for r in t.readers:
                deps.add(r)
        for d in sorted(deps, key=lambda o: -o.idx):
            if d is op:
                continue
            if d.is_dma:
                if d.idx in kd:
                    continue
                op.waits.append(d)
                d.signal = True
                d.waited = True
                kd.add(d.idx)
                kd |= d.dma_known
                for f in ENGINES:
                    if d.clock[f] > known[f]:
                        known[f] = d.clock[f]
            else:
                if d.eng == eng and not (t_is_raw(d, reads)):
                    continue
                if d.eidx <= known[d.eng]:
                    continue
                op.waits.append(d)
                d.signal = True
                for f in ENGINES:
                    if d.clock[f] > known[f]:
                        known[f] = d.clock[f]
                kd |= d.dma_known
                known[d.eng] = max(known[d.eng], d.eidx)
        if is_cc:
            op.cc = self.ncc
            self.ncc += 1
            op.signal = True
        elif is_dma:
            slot = self.dma_count % N_DMA_SEMS
            prev = self.dma_slot_last[slot]
            if prev is not None and prev.idx not in kd:
                op.prewait = prev
                prev.signal = True
                prev.waited = True
                kd.add(prev.idx)
            op.dma_slot = slot
            op.dma_val = 16 * (self.dma_count // N_DMA_SEMS + 1)
            self.dma_slot_last[slot] = op
            self.dma_count += 1
            op.signal = True
        op.clock = dict(known)
        if not is_dma:
            op.clock[eng] = max(op.clock[eng], -1)
        op.dma_known = set(kd)
        for t in reads:
            t.readers.append(op)
        for t in writes:
            t.last_w = op
            t.readers = []
        self.ops.append(op)
        self.eng_ops[eng].append(op)
        return op

    def op(self, eng, fn, reads=(), writes=()):
        reads, writes = list(reads), list(writes)
        for t in reads:
            if t.excl and t not in writes:
                writes.append(t)
        return self._add(eng, fn, reads, writes, False)

    def dma(self, eng, out, in_, reads=(), writes=()):
        def fn(e):
            return e.dma_start(out=out, in_=in_)
        return self._add(eng, fn, list(reads), list(writes), True)

    def barrier(self):
        lasts = []
        for e in ENGINES:
            nd = [o for o in self.eng_ops[e] if (not o.is_dma) and o.fn is not None]
            if nd:
                lasts.append(nd[-1])
        dmas = [o for o in self.ops if o.is_dma and not o.waited]
        for e in ENGINES:
            toks = []
            for l in lasts + dmas:
                tt = Tok("b")
                tt.last_w = l
                toks.append(tt)
            self._add(e, None, toks, [], False)

    def emit(self):
        nc = self.nc
        sems = {}
        for e in ENGINES:
            sems[e] = self.stack.enter_context(nc.semaphore("s_" + e))
        dsem = [self.stack.enter_context(nc.semaphore(f"d{i}")) for i in range(N_DMA_SEMS)]
        csem = [self.stack.enter_context(nc.semaphore(f"cc{i}")) for i in range(self.ncc)]
        for e in ENGINES:
            c = 0
            for o in self.eng_ops[e]:
                if o.is_dma:
                    continue
                if o.signal and o.fn is None:
                    raise RuntimeError("barrier op cannot signal")
                if o.signal:
                    c += 1
                o.dma_val = c if not o.is_dma else o.dma_val
        block = self.stack.enter_context(nc.Block())
        engmap = {"pe": block.tensor, "act": block.scalar, "dve": block.vector,
                  "pool": block.gpsimd, "sp": block.sync}

        def make(ename):
            def body(e):
                for o in self.eng_ops[ename]:
                    if o.prewait is not None:
                        e.wait_ge(dsem[o.prewait.dma_slot], o.prewait.dma_val)
                    for d in o.waits:
                        if d.cc is not None:
                            e.wait_ge(csem[d.cc], 1)
                        elif d.is_dma:
                            e.wait_ge(dsem[d.dma_slot], d.dma_val)
                        else:
                            e.wait_ge(sems[d.eng], d.dma_val)
                    if o.fn is None:
                        continue
                    ins = o.fn(e)
                    if o.cc is not None:
                        ins.then_inc(csem[o.cc])
                    elif o.is_dma:
                        ins.then_inc(dsem[o.dma_slot], 16)
                    elif o.signal:
                        ins.then_inc(sems[ename], 1)
            return body

        for ename in ENGINES:
            if self.eng_ops[ename]:
                engmap[ename](make(ename))
        self.stack.close()
        return nc


def t_is_raw(d, reads):
    for t in reads:
        if t.last_w is d:
            return True
    return False


class Ctx:
    pass


def setup_common(P, ntok=TOK):
    C = Ctx()
    C.P = P
    C.ntok = ntok
    C.nblk = ntok // 512
    C.x = P.sbuf("x", [128, KD, ntok], F32)
    C.xt = [P.tok(f"x{b}") for b in range(C.nblk)]
    C.hn = P.sbuf("hn", [128, KD, ntok], BF16)
    C.hnt = [P.tok(f"hn{b}") for b in range(C.nblk)]
    C.ones = P.sbuf("ones", [128, 128], F32)
    C.ones_t = P.tok("ones")
    P.op("pool", lambda e: e.memset(C.ones[:], 1.0), [], [C.ones_t])
    C.sq = [P.sbuf(f"sq{i}", [128, 512], F32) for i in range(2)]
    C.sq_t = [P.tok() for i in range(2)]
    C.arena = P.sbuf("arenaC", [128, 12 * 1024], F32)
    C.rstd = P.sbuf("rstd", [128, 512], F32)
    C.rstd_t = P.tok("rstd")
    bs, ts = P.psum_banks()
    C.pst, C.pst_t = bs[0], ts[0]
    C.pg, C.pg_t = bs[1:3], ts[1:3]
    C.pu, C.pu_t = bs[3:5], ts[3:5]
    C.po, C.po_t = bs[5:7], ts[5:7]
    C.cnt = 0
    return C


def load_vec(P, C, name, dram_ap, n):
    t = P.sbuf(name, [128, n], F32)
    tk = P.tok(name)
    P.dma("sp", t[:], dram_ap, [], [tk])
    return t, tk


def rmsnorm_block(P, C, b, gain, gain_t, out, out_t, out_is_f32=False):
    sl = slice(b * 512, (b + 1) * 512)
    for k in range(KD):
        sq, sq_t = C.sq[k % 2], C.sq_t[k % 2]
        P.op("act", lambda e, k=k, sq=sq: e.activation(out=sq[:], in_=C.x[:, k, sl], func=AF.Square),
             [C.xt[b]], [sq_t])
        P.op("pe", lambda e, k=k, sq=sq: e.matmul(C.pst[:], lhsT=C.ones[:], rhs=sq[:],
                                                 start=(k == 0), stop=(k == KD - 1)),
             [sq_t, C.ones_t], [C.pst_t])
    P.op("act", lambda e: e.activation(out=C.rstd[:], in_=C.pst[:], func=AF.Sqrt,
                                       scale=1.0 / D_MODEL, bias=C.epsb[:]),
         [C.pst_t, C.epsb_t], [C.rstd_t])
    P.op("dve", lambda e: e.reciprocal(out=C.rstd[:], in_=C.rstd[:]), [C.rstd_t], [C.rstd_t])
    for k in range(KD):
        P.op("dve", lambda e, k=k: e.scalar_tensor_tensor(
            out=out[:, k, sl], in0=C.x[:, k, sl], scalar=gain[:, k:k + 1], in1=C.rstd[:],
            op0=ALU.mult, op1=ALU.mult), [C.xt[b], gain_t, C.rstd_t], [out_t[b]])


def ffn_alloc(P, C, nsplit=2):
    nblk = C.nblk
    per = NFF // nsplit
    if True:
        C.act = C.arena[:, 0:per * C.ntok // 2].bitcast(BF16).rearrange("p (c t) -> p c t", c=per)
        C.act_t = [[P.tok() for b in range(nblk)] for c in range(per)]
        C.wst = [P.sbuf(f"wst{i}", [128, 2, KD, 128], F32) for i in range(2)]
        C.wst_t = [P.tok() for i in range(2)]
        C.wb = [P.sbuf(f"wb{i}", [128, 2, KD, 128], BF16) for i in range(2)]
        C.wb_t = [P.tok() for i in range(2)]
        C.wdst = [P.sbuf(f"wdst{i}", [128, per, 128], F32) for i in range(2)]
        C.wdst_t = [P.tok() for i in range(2)]
        C.wdb = [P.sbuf(f"wdb{i}", [128, per, 128], BF16) for i in range(2)]
        C.wdb_t = [P.tok() for i in range(2)]
        C.sg = [P.sbuf(f"sg{i}", [128, 512], F32) for i in range(2)]
        C.sg_t = [P.tok() for i in range(2)]
        C.wcnt = 0
        C.wdcnt = 0
        C.gcnt = 0
        C.ocnt = 0


def ffn(P, C, wg_d, wu_d, wd_d, nsplit=2):
    nblk = C.nblk
    per = NFF // nsplit
    if not hasattr(C, "act"):
        ffn_alloc(P, C, nsplit)
    for h in range(nsplit):
        for cl in range(per):
            c = h * per + cl
            i = C.wcnt % 2
            C.wcnt += 1
            P.dma("sp", C.wst[i][:, 0], wg_d[c], [], [C.wst_t[i]])
            P.dma("sp", C.wst[i][:, 1], wu_d[c], [], [C.wst_t[i]])
            P.op("pool", lambda e, i=i: e.tensor_copy(out=C.wb[i][:], in_=C.wst[i][:]),
                 [C.wst_t[i]], [C.wb_t[i]])
            for b in range(nblk):
                sl = slice(b * 512, (b + 1) * 512)
                j = C.gcnt % 2
                C.gcnt += 1
                for k in range(KD):
                    P.op("pe", lambda e, i=i, j=j, k=k, sl=sl: e.matmul(
                        C.pg[j][:], lhsT=C.wb[i][:, 0, k, :], rhs=C.hn[:, k, sl],
                        start=(k == 0), stop=(k == KD - 1)), [C.wb_t[i], C.hnt[b]], [C.pg_t[j]])
                for k in range(KD):
                    P.op("pe", lambda e, i=i, j=j, k=k, sl=sl: e.matmul(
                        C.pu[j][:], lhsT=C.wb[i][:, 1, k, :], rhs=C.hn[:, k, sl],
                        start=(k == 0), stop=(k == KD - 1)), [C.wb_t[i], C.hnt[b]], [C.pu_t[j]])
                P.op("act", lambda e, j=j: e.activation(out=C.sg[j][:], in_=C.pg[j][:], func=AF.Silu),
                     [C.pg_t[j]], [C.sg_t[j]])
                P.op("dve", lambda e, j=j, cl=cl, sl=sl: e.tensor_tensor(
                    out=C.act[:, cl, sl], in0=C.sg[j][:], in1=C.pu[j][:], op=ALU.mult),
                    [C.sg_t[j], C.pu_t[j]], [C.act_t[cl][b]])
        for m in range(KD):
            i = C.wdcnt % 2
            C.wdcnt += 1
            P.dma("sp", C.wdst[i][:], wd_d[m, :, h * per:(h + 1) * per, :], [], [C.wdst_t[i]])
            P.op("pool", lambda e, i=i: e.tensor_copy(out=C.wdb[i][:], in_=C.wdst[i][:]),
                 [C.wdst_t[i]], [C.wdb_t[i]])
            for b in range(nblk):
                sl = slice(b * 512, (b + 1) * 512)
                j = C.ocnt % 2
                C.ocnt += 1
                for cl in range(per):
                    P.op("pe", lambda e, i=i, j=j, cl=cl, sl=sl: e.matmul(
                        C.po[j][:], lhsT=C.wdb[i][:, cl, :], rhs=C.act[:, cl, sl],
                        start=(cl == 0), stop=(cl == per - 1)),
                        [C.wdb_t[i], C.act_t[cl][b]], [C.po_t[j]])
                P.op("dve", lambda e, j=j, m=m, sl=sl: e.scalar_tensor_tensor(
                    out=C.x[:, m, sl], in0=C.po[j][:], scalar=0.5, in1=C.x[:, m, sl],
                    op0=ALU.mult, op1=ALU.add), [C.po_t[j], C.xt[b]], [C.xt[b]])


def load_x(P, C, x_d):
    for k in range(KD):
        for b in range(C.nblk):
            sl = slice(b * 512, (b + 1) * 512)
            P.dma("sp", C.x[:, k, sl], x_d[k, :, sl], [], [C.xt[b]])


def store_feat(P, C, src, src_t, dst_d, eng="pool"):
    toks = []
    for k in range(KD):
        t = P.tok()
        P.dma(eng, dst_d[k], src[:, k, :], list(src_t), [t])
        toks.append(t)
    return toks


def eps_const(P, C):
    C.epsb = P.sbuf("epsb", [128, 1], F32)
    C.epsb_t = P.tok("epsb")
    P.op("pool", lambda e: e.memset(C.epsb[:], EPS), [], [C.epsb_t])


def build_A():
    P = Prog()
    if os.environ.get("MEGA_A"):
        P.enable_mega(MEGA_WORDS)
    for i in range(int(os.environ.get("EXTRA_IN", "0"))):
        P.dram(f"extra{i}", [128, 2], F32, "ExternalInput")
    for i in range(int(os.environ.get("EXTRA_INT", "0"))):
        P.dram(f"extraint{i}", [128, 2048], BF16, "Internal")
    x_d = P.dram("xT", [KD, 128, TOK], F32, "ExternalInput")
    wg_d = P.dram("wg", [NFF, 128, KD, 128], F32, "ExternalInput")
    wu_d = P.dram("wu", [NFF, 128, KD, 128], F32, "ExternalInput")
    wd_d = P.dram("wd", [KD, 128, NFF, 128], F32, "ExternalInput")
    g1_d = P.dram("g_ffn", [128, KD], F32, "ExternalInput")
    g2_d = P.dram("g_mix", [128, KD], F32, "ExternalInput")
    xo_d = P.dram("xo", [KD, 128, TOK], F32, "ExternalOutput")
    hn_d = P.dram("hno", [KD, 128, TOK], BF16, "ExternalOutput")
    C = setup_common(P)
    eps_const(P, C)
    g1, g1t = load_vec(P, C, "g1", g1_d, KD)
    g2, g2t = load_vec(P, C, "g2", g2_d, KD)
    load_x(P, C, x_d)
    for b in range(C.nblk):
        rmsnorm_block(P, C, b, g1, g1t, C.hn, C.hnt)
    ffn(P, C, wg_d, wu_d, wd_d)
    for b in range(C.nblk):
        rmsnorm_block(P, C, b, g2, g2t, C.hn, C.hnt)
    store_feat(P, C, C.x, C.xt, xo_d)
    store_feat(P, C, C.hn, C.hnt, hn_d)
    P.barrier()
    return P.emit()


def feat_major(a):
    t = np.ascontiguousarray(a.T)
    return t.reshape(t.shape[0] // 128, 128, t.shape[1])


def tile_w_kc(w):
    K, N = w.shape
    return np.ascontiguousarray(w.reshape(K // 128, 128, N // 128, 128).transpose(2, 1, 0, 3))


def vec128(v):
    return np.ascontiguousarray(v.reshape(-1, 128).T)


NB = SEQ // 512
NT = SEQ // 128
ARENA_W = 27 * 1024
WCOLS = 1056


class Arena:
    def __init__(self, P):
        self.t = P.sbuf("arena", [128, ARENA_W], F32)
        if P.mega is not None:
            self.t = self.t
        self.off = 0

    def reset(self):
        self.off = 0

    def take(self, nbytes, dtype, shape=None, parts=128):
        words = (nbytes + 3) // 4
        assert self.off + words <= ARENA_W, (self.off, words)
        ap = self.t[0:parts, self.off:self.off + words]
        self.off += words
        if dtype != F32:
            ap = ap.bitcast(dtype)
        return ap


def load_weights(P, B, w_d, ncols):
    c0 = 0
    while c0 < ncols:
        n = min(256, ncols - c0)
        i = B.wcnt % 2
        B.wcnt += 1
        P.dma("sp", B.wst[i][:, :, 0:n], w_d[:, :, c0:c0 + n], [], [B.wst_t[i]])
        P.op("pool", lambda e, i=i, n=n, c0=c0: e.tensor_copy(out=B.wbf[:, :, c0:c0 + n],
                                                             in_=B.wst[i][:, :, 0:n]),
             [B.wst_t[i]], [B.wbf_t])
        c0 += n


def proj_feat(P, B, col0, m, blk, ps, ps_t, n=512, tok0=None):
    t0 = blk * 512 if tok0 is None else tok0
    for k in range(KD):
        P.op("pe", lambda e, k=k: e.matmul(ps[0:m, 0:n], lhsT=B.wbf[:, k, col0:col0 + m],
                                         rhs=B.hn[:, k, t0:t0 + n],
                                         start=(k == 0), stop=(k == KD - 1)),
             [B.wbf_t, B.hn_t] + B.hn_toks, [ps_t])


def proj_tok(P, B, col0, n, tile, ps, ps_t):
    t0 = tile * 128
    for k in range(KD):
        P.op("pe", lambda e, k=k: e.matmul(ps[:, 0:n], lhsT=B.hn[:, k, t0:t0 + 128],
                                         rhs=B.wbf[:, k, col0:col0 + n],
                                         start=(k == 0), stop=(k == KD - 1)),
             [B.wbf_t, B.hn_t] + B.hn_toks, [ps_t])


def setup_B(P):
    B = Ctx()
    B.P = P
    B.hn = P.sbuf("hnB", [128, KD, SEQ], BF16)
    B.hn_t = P.tok("hnB")
    B.hn_toks = []
    B.wst = [P.sbuf(f"wstB{i}", [128, KD, 256], F32) for i in range(2)]
    B.wst_t = [P.tok() for i in range(2)]
    B.wbf = P.sbuf("wbfB", [128, KD, WCOLS], BF16)
    B.wbf_t = P.tok("wbf")
    B.wcnt = 0
    B.yout_t = P.tok("yout")
    B.ar = Arena(P)
    B.ps, B.ps_t = P.psum_banks()
    B.ones = P.sbuf("onesB", [128, 128], F32)
    B.ones_t = P.tok("onesB")
    P.op("pool", lambda e: e.memset(B.ones[:], 1.0), [], [B.ones_t])
    B.ones2 = P.sbuf("ones2B", [128, 128], F32)
    B.ones2_t = P.tok("ones2B")
    P.op("pool", lambda e: e.memset(B.ones2[:], 0.0), [], [B.ones2_t])
    P.op("pool", lambda e: e.memset(B.ones2[0:64, 0:64], 1.0), [], [B.ones2_t])
    P.op("pool", lambda e: e.memset(B.ones2[64:128, 64:128], 1.0), [], [B.ones2_t])
    B.epsb = P.sbuf("epsB", [128, 2], F32)
    B.epsb_t = P.tok("epsB")
    P.op("pool", lambda e: e.memset(B.epsb[:, 0:1], EPS), [], [B.epsb_t])
    P.op("pool", lambda e: e.memset(B.epsb[:, 1:2], 64.0 * EPS), [], [B.epsb_t])
    B.oneb = P.sbuf("oneB", [128, 1], F32)
    B.oneb_t = P.tok("oneB")
    P.op("pool", lambda e: e.memset(B.oneb[:], 1.0), [], [B.oneb_t])
    return B


def attn_branch(P, B, w_d, cos_d, sin_d, gq_d, gk_d, out_d):
    ar = B.ar
    ar.reset()
    cos = ar.take(SEQ * 4, F32)
    sin = ar.take(SEQ * 4, F32)
    QT = ar.take(2 * SEQ * 2, BF16).rearrange("p (a t) -> p a t", a=2)
    KT = ar.take(SEQ * 2, BF16)
    VA = [ar.take(NT * 128 * 2, BF16).rearrange("p (n c) -> p n c", c=128) for _ in range(2)]
    YA = ar.take(2 * SEQ * 2, BF16).rearrange("p (a t) -> p a t", a=2)
    PT = [ar.take(512 * 2, BF16) for _ in range(3)]
    sq = ar.take(512 * 4, F32)
    rr = ar.take(512 * 4, F32)
    t1 = ar.take(512 * 4, F32)
    t2 = ar.take(512 * 4, F32)
    rc = ar.take(512 * 4, F32)
    gq = ar.take(8, F32)
    gk = ar.take(8, F32)
    tk = lambda n: P.tok(n)
    cos_t, sin_t, QT_t, KT_t, YA_t = tk("cos"), tk("sin"), tk("QT"), tk("KT"), tk("YA")
    VA_t = [tk("va0"), tk("va1")]
    PT_t = [tk("pt") for _ in range(3)]
    sq_t, rr_t, t1_t, t2_t, rc_t, gq_t, gk_t = (tk(n) for n in ("sq", "rr", "t1", "t2", "rc", "gq", "gk"))
    P.dma("sp", cos, cos_d, [], [cos_t])
    P.dma("sp", sin, sin_d, [], [sin_t])
    P.dma("sp", gq, gq_d, [], [gq_t])
    P.dma("sp", gk, gk_d, [], [gk_t])
    load_weights(P, B, w_d, 832)
    P.op("pool", lambda e: e.memset(VA[0][:, :, 64:128], 1.0), [], [VA_t[0]])
    P.op("pool", lambda e: e.memset(VA[1][:, :, 0:64], 1.0), [], [VA_t[1]])
    psA, psA_t = B.ps, B.ps_t

    def qk_prep(col, colrot, g, g_t, dst, dst_t, is_q, blk):
        sl = slice(blk * 512, (blk + 1) * 512)
        pa, pb, pc = psA[0], psA[1], psA[2]
        proj_feat(P, B, col, 128, blk, pa, psA_t[0])
        proj_feat(P, B, colrot, 128, blk, pb, psA_t[1])
        P.op("act", lambda e: e.activation(out=sq, in_=pa[:], func=AF.Square), [psA_t[0]], [sq_t])
        P.op("pe", lambda e: e.matmul(pc[:], lhsT=B.ones2[:], rhs=sq, start=True, stop=True),
             [sq_t, B.ones2_t], [psA_t[2]])
        if is_q:
            P.op("act", lambda e: e.activation(out=rr, in_=pc[:], func=AF.Sqrt, scale=1.0,
                                               bias=B.epsb[:, 1:2]), [psA_t[2], B.epsb_t], [rr_t])
        else:
            P.op("act", lambda e: e.activation(out=rr, in_=pc[:], func=AF.Sqrt, scale=1.0 / 64,
                                               bias=B.epsb[:, 0:1]), [psA_t[2], B.epsb_t], [rr_t])
        P.op("dve", lambda e: e.reciprocal(out=rr, in_=rr), [rr_t], [rr_t])
        P.op("dve", lambda e: e.scalar_tensor_tensor(out=t1, in0=pa[:], scalar=g[:, 0:1],
                                                     in1=cos[:, sl], op0=ALU.mult, op1=ALU.mult),
             [psA_t[0], g_t, cos_t], [t1_t])
        P.op("dve", lambda e: e.scalar_tensor_tensor(out=t2, in0=pb[:], scalar=g[:, 1:2],
                                                     in1=sin[:, sl], op0=ALU.mult, op1=ALU.mult),
             [psA_t[1], g_t, sin_t], [t2_t])
        P.op("dve", lambda e: e.tensor_tensor(out=t1, in0=t1, in1=t2, op=ALU.add), [t1_t, t2_t], [t1_t])
        P.op("dve", lambda e: e.tensor_tensor(out=dst[:, sl], in0=t1, in1=rr, op=ALU.mult),
             [t1_t, rr_t], [dst_t])

    for blk in range(NB):
        for pair in range(2):
            qk_prep(pair * 128, 256 + pair * 128, gq, gq_t, QT[:, pair, :], QT_t, True, blk)
        qk_prep(512, 640, gk, gk_t, KT, KT_t, False, blk)
    for tile in range(NT):
        pv = psA[3 + tile % 2]
        pv_t = psA_t[3 + tile % 2]
        proj_tok(P, B, 768, 64, tile, pv, pv_t)
        P.op("act", lambda e, tile=tile, pv=pv: e.copy(out=VA[0][:, tile, 0:64], in_=pv[:, 0:64]),
             [pv_t], [VA_t[0]])
        P.op("dve", lambda e, tile=tile, pv=pv: e.tensor_copy(out=VA[1][:, tile, 64:128], in_=pv[:, 0:64]),
             [pv_t], [VA_t[1]])
    po = [psA[6], psA[7]]
    po_t = [psA_t[6], psA_t[7]]
    def do_block(pair, j, qb, o, o_t):
        rows = slice(64 * j, 64 * j + 64)
        orow = slice(64 * (1 - j), 64 * (1 - j) + 64)
        qs = slice(qb * 512, (qb + 1) * 512)

        def qk(kt):
            s = psA[kt % 3]
            P.op("pe", lambda e: e.matmul(s[:], lhsT=KT[rows, kt * 128:(kt + 1) * 128],
                                          rhs=QT[rows, pair, qs], start=True, stop=True),
                 [KT_t, QT_t], [psA_t[kt % 3]])

        def step(kt):
            s = psA[kt % 3]
            pt = PT[kt % 3]
            P.op("act", lambda e: e.activation(out=pt, in_=s[:], func=AF.Exp),
                 [psA_t[kt % 3]], [PT_t[kt % 3]])
            if kt + 2 < NT:
                qk(kt + 2)
            P.op("pe", lambda e: e.matmul(o[:], lhsT=VA[j][:, kt, :], rhs=pt,
                                          start=(kt == 0), stop=(kt == NT - 1)),
                 [VA_t[j], PT_t[kt % 3]], [o_t])

        qk(0)
        qk(1)
        for kt in range(NT):
            step(kt)
        P.op("act", lambda e: e.copy(out=rc[rows, :], in_=o[orow, :]), [o_t], [rc_t])
        P.op("dve", lambda e: e.reciprocal(out=rc[rows, :], in_=rc[rows, :]), [rc_t], [rc_t])
        P.op("dve", lambda e: e.tensor_tensor(out=YA[rows, pair, qs], in0=o[rows, :],
                                              in1=rc[rows, :], op=ALU.mult),
             [o_t, rc_t], [YA_t])

    it = 0
    import os
    stage = int(os.environ.get("ATTN_STAGE", "9"))
    for pair in range(2):
        for j in range(2):
            for qb in range(NB):
                if stage == 0 or (stage == 1 and it >= 1):
                    continue
                do_block(pair, j, qb, po[it % 2], po_t[it % 2])
                it += 1
    for pair in range(2):
        for h in range(2):
            P.dma("pool", out_d[pair][h], YA[:, pair, h * TOK:(h + 1) * TOK], [YA_t], [B.yout_t])


def gla_branch(P, B, w_d, wal_d, nb_d, gn_d, cst_d, out_d):
    ar = B.ar
    ps, ps_t = B.ps, B.ps_t
    load_weights(P, B, w_d, 1056)
    tk = lambda n: P.tok(n)
    def one_head(hh):
        ar.reset()
        qd = [ar.take(SEQ * 2, BF16) for _ in range(2)]
        kd = [ar.take(SEQ * 2, BF16) for _ in range(2)]
        Sb = ar.take(2 * NT * 128 * 2, BF16).rearrange("p (d n e) -> p d n e", d=2, n=NT)
        vt = ar.take(NT * 128 * 2, BF16).rearrange("p (n e) -> p n e", e=128)
        GS = ar.take(SEQ * 2, BF16)
        YG = ar.take(SEQ * 2, BF16)
        tmp = [ar.take(512 * 4, F32) for _ in range(8)]
        e1, sp, Bc, bb, Eq, Ek, ysq, yt = tmp
        zT = ar.take(512 * 2, BF16)
        S = ar.take(128 * 4, F32)
        kvs2 = [ar.take(128 * 4, F32) for _ in range(2)]
        kdt2 = [ar.take(128 * 2, BF16) for _ in range(2)]
        kvs2_t = [P.tok("kvs0"), P.tok("kvs1")]
        kdt2_t = [P.tok("kdt0"), P.tok("kdt1")]
        Pm = [ar.take(512 * 2, BF16) for _ in range(2)]
        dec = ar.take(2 * NT * 4, F32).rearrange("p (d n) -> p d n", d=2)
        m01 = ar.take(512 * 4, F32)
        cst = ar.take(3 * 128 * 4, F32).rearrange("p (a c) -> p a c", a=3)
        identb = ar.take(128 * 2, BF16)
        wal = ar.take(2 * 256 * 4, F32).rearrange("p (d c) -> p d c", d=2)
        walb = ar.take(2 * 256 * 2, BF16).rearrange("p (d c) -> p d c", d=2)
        ar.take(64, F32)
        nb = ar.take(64, F32)[:, 0:4].rearrange("p (d h) -> p d h", d=2)
        gn = ar.take(64, F32)[:, 0:1]
        ar.take(64, F32)
        rr = ar.take(512 * 4, F32)
        qd_t = [tk("qd0"), tk("qd1")]
        kd_t = [tk("kd0"), tk("kd1")]
        Sb_t = [tk("sb0"), tk("sb1")]
        (vt_t, GS_t, YG_t, e1_t, sp_t, Bc_t, bb_t, Eq_t, Ek_t, ysq_t, yt_t, zT_t, S_t, kvs_t, kdt_t,
         dec_t, m01_t, cst_t, identb_t, wal_t, walb_t, nb_t, gn_t, rr_t) = (tk("g") for _ in range(24))
        Pm_t = [tk("pm0"), tk("pm1")]
        P.dma("sp", cst, cst_d, [], [cst_t])
        P.dma("sp", wal[0:32], wal_d, [], [wal_t])
        P.dma("sp", nb, nb_d, [], [nb_t])
        P.dma("sp", gn, gn_d, [], [gn_t])
        small = [cst_t, wal_t, nb_t, gn_t]
        P.op("pool", lambda e: e.tensor_copy(out=walb[0:32], in_=wal[0:32]), small, [walb_t])
        P.op("pool", lambda e: e.tensor_copy(out=identb, in_=cst[:, 2, :]), small, [identb_t])
        P.op("dve", lambda e: e.tensor_scalar(out=nb, in0=nb, scalar1=-1.0, scalar2=None, op0=ALU.mult),
             small, [nb_t])
        P.op("pool", lambda e: e.memset(m01, 1.0), [], [m01_t])
        P.op("pool", lambda e: e.memset(m01.rearrange("p (c i) -> p c i", i=128)[:, :, 0:1], 0.0),
             [], [m01_t])
        P.op("pool", lambda e: e.memset(S, 0.0), [], [S_t])
        scale_q = 128.0 ** -0.5

        def prep_blk(blk):
            sl = slice(blk * 512, (blk + 1) * 512)
            pz, pz_t = ps[0], ps_t[0]
            proj_feat(P, B, 1024, 32, blk, pz, pz_t)
            P.op("act", lambda e: e.copy(out=zT[0:32], in_=pz[0:32, :]), [pz_t], [zT_t])
            pq, pq_t = ps[1], ps_t[1]
            pk, pk_t = ps[2], ps_t[2]
            pgt, pgt_t = ps[3], ps_t[3]
            proj_feat(P, B, 0 + hh * 128, 128, blk, pq, pq_t)
            proj_feat(P, B, 256 + hh * 128, 128, blk, pk, pk_t)
            proj_feat(P, B, 768 + hh * 128, 128, blk, pgt, pgt_t)
            P.op("act", lambda e: e.activation(out=GS[:, sl], in_=pgt[:], func=AF.Silu), [pgt_t], [GS_t])
            def prep_dir(d):
                pl, pl_t = ps[4 + d], ps_t[4 + d]
                P.op("pe", lambda e: e.matmul(pl[:], lhsT=walb[0:32, d, hh * 128:(hh + 1) * 128],
                                              rhs=zT[0:32], start=True, stop=True),
                     [walb_t, zT_t], [pl_t])
                P.op("act", lambda e: e.activation(out=e1, in_=pl[:], func=AF.Exp, scale=-1.0,
                                                   bias=nb[:, d, hh:hh + 1]), [pl_t, nb_t], [e1_t])
                P.op("act", lambda e: e.activation(out=sp, in_=e1, func=AF.Ln, scale=1.0, bias=B.oneb[:]),
                     [e1_t, B.oneb_t], [sp_t])
                P.op("dve", lambda e: e.tensor_tensor_scan(out=Bc, data0=m01, data1=sp, initial=0.0,
                                                           op0=ALU.mult, op1=ALU.add),
                     [m01_t, sp_t], [Bc_t])
                if d == 0:
                    src, src_t = Bc, Bc_t
                else:
                    P.op("dve", lambda e: e.tensor_tensor(out=bb, in0=sp, in1=Bc, op=ALU.subtract),
                         [sp_t, Bc_t], [bb_t])
                    b3 = bb.rearrange("p (c i) -> p c i", i=128)
                    tot = Bc.rearrange("p (c i) -> p c i", i=128)[:, :, 127:128]
                    P.op("dve", lambda e: e.tensor_tensor(out=b3, in0=b3, in1=tot.to_broadcast([128, 4, 128]),
                                                          op=ALU.add), [bb_t, Bc_t], [bb_t])
                    src, src_t = bb, bb_t
                P.op("act", lambda e: e.activation(out=Eq, in_=src, func=AF.Exp, scale=-1.0 / 16),
                     [src_t], [Eq_t])
                P.op("act", lambda e: e.activation(out=Ek, in_=src, func=AF.Exp, scale=1.0 / 16),
                     [src_t], [Ek_t])
                P.op("dve", lambda e: e.scalar_tensor_tensor(out=qd[d][:, sl], in0=pq[:], scalar=scale_q,
                                                             in1=Eq, op0=ALU.mult, op1=ALU.mult),
                     [pq_t, Eq_t], [qd_t[d]])
                P.op("dve", lambda e: e.tensor_tensor(out=kd[d][:, sl], in0=pk[:], in1=Ek, op=ALU.mult),
                     [pk_t, Ek_t], [kd_t[d]])
                col = 127 if d == 0 else 0
                e3 = Eq.rearrange("p (c i) -> p c i", i=128)[:, :, col]
                P.op("dve", lambda e: e.tensor_copy(out=dec[:, d, blk * 4:(blk + 1) * 4], in_=e3),
                     [Eq_t], [dec_t])

            prep_dir(0)
            prep_dir(1)

        for blk in range(NB):
            prep_blk(blk)

        def v_tile(tile):
            pv, pv_t = ps[6 + tile % 2], ps_t[6 + tile % 2]
            proj_tok(P, B, 512 + hh * 128, 128, tile, pv, pv_t)
            P.op("act", lambda e: e.copy(out=vt[:, tile, :], in_=pv[:, 0:128]), [pv_t], [vt_t])

        for tile in range(NT):
            v_tile(tile)

        def state_step(d, n, k):
            ptr = ps[0 + k % 2][:, 0:64].bitcast(BF16)
            ptr_t = ps_t[0 + k % 2]
            pkv, pkv_t = ps[2 + k % 2], ps_t[2 + k % 2]
            cs = slice(n * 128, (n + 1) * 128)
            kdt, kdt_t, kvs, kvs_t = kdt2[k % 2], kdt2_t[k % 2], kvs2[k % 2], kvs2_t[k % 2]
            P.op("pe", lambda e: e.transpose(ptr, kd[d][:, cs], identb), [kd_t[d], identb_t], [ptr_t])
            P.op("dve", lambda e: e.tensor_copy(out=kdt, in_=ptr), [ptr_t], [kdt_t])
            P.op("pe", lambda e: e.matmul(pkv[:, 0:128], lhsT=kdt, rhs=vt[:, n, :], start=True, stop=True),
                 [kdt_t, vt_t], [pkv_t])
            P.op("act", lambda e: e.copy(out=Sb[:, d, n, :], in_=S), [S_t], [Sb_t[d]])
            P.op("dve", lambda e: e.tensor_scalar(out=kvs, in0=pkv[:, 0:128], scalar1=dec[:, d, n:n + 1],
                                                  scalar2=None, op0=ALU.mult), [pkv_t, dec_t], [kvs_t])
            P.op("dve", lambda e: e.scalar_tensor_tensor(out=S, in0=S, scalar=dec[:, d, n:n + 1], in1=kvs,
                                                         op0=ALU.mult, op1=ALU.add),
                 [S_t, dec_t, kvs_t], [S_t])

        k = 0
        for n in range(NT):
            state_step(0, n, k)
            k += 1
        P.op("pool", lambda e: e.memset(S, 0.0), [S_t], [S_t])
        for n in range(NT - 1, -1, -1):
            state_step(1, n, k)
            k += 1

        def out_blk(blk):
            sl = slice(blk * 512, (blk + 1) * 512)
            psc = [ps[4], ps[5]]
            psc_t = [ps_t[4], ps_t[5]]
            po, po_t = ps[6 + blk % 2], ps_t[6 + blk % 2]
            for d in range(2):
                for c in range(4):
                    n = blk * 4 + c
                    cs = slice(n * 128, (n + 1) * 128)
                    lc = slice(c * 128, (c + 1) * 128)
                    P.op("pe", lambda e, d=d, cs=cs, lc=lc: e.matmul(
                        psc[d][:, lc], lhsT=kd[d][:, cs], rhs=qd[d][:, cs], start=True, stop=True),
                        [kd_t[d], qd_t[d]], [psc_t[d]])
                for c in range(4):
                    lc = slice(c * 128, (c + 1) * 128)
                    P.op("dve", lambda e, d=d, lc=lc: e.tensor_tensor(
                        out=Pm[d][:, lc], in0=psc[d][:, lc], in1=cst[:, d, :], op=ALU.mult),
                        [psc_t[d], cst_t], [Pm_t[d]])
            for c in range(4):
                n = blk * 4 + c
                cs = slice(n * 128, (n + 1) * 128)
                lc = slice(c * 128, (c + 1) * 128)
                P.op("pe", lambda e, n=n, lc=lc: e.matmul(po[:, lc], lhsT=vt[:, n, :], rhs=Pm[0][:, lc],
                                                          start=True, stop=False),
                     [vt_t, Pm_t[0]], [po_t])
                P.op("pe", lambda e, n=n, lc=lc, cs=cs: e.matmul(po[:, lc], lhsT=Sb[:, 0, n, :],
                                                                 rhs=qd[0][:, cs], start=False, stop=False),
                     [Sb_t[0], qd_t[0]], [po_t])
                P.op("pe", lambda e, n=n, lc=lc: e.matmul(po[:, lc], lhsT=vt[:, n, :], rhs=Pm[1][:, lc],
                                                          start=False, stop=False),
                     [vt_t, Pm_t[1]], [po_t])
                P.op("pe", lambda e, n=n, lc=lc, cs=cs: e.matmul(po[:, lc], lhsT=Sb[:, 1, n, :],
                                                                 rhs=qd[1][:, cs], start=False, stop=True),
                     [Sb_t[1], qd_t[1]], [po_t])
            pst, pst_t = ps[0 + blk % 2], ps_t[0 + blk % 2]
            P.op("act", lambda e: e.activation(out=ysq, in_=po[:], func=AF.Square), [po_t], [ysq_t])
            P.op("pe", lambda e: e.matmul(pst[:], lhsT=B.ones[:], rhs=ysq, start=True, stop=True),
                 [ysq_t, B.ones_t], [pst_t])
            P.op("act", lambda e: e.activation(out=rr, in_=pst[:], func=AF.Sqrt, scale=1.0 / 128,
                                               bias=B.epsb[:, 0:1]), [pst_t, B.epsb_t], [rr_t])
            P.op("dve", lambda e: e.reciprocal(out=rr, in_=rr), [rr_t], [rr_t])
            P.op("dve", lambda e: e.scalar_tensor_tensor(out=yt, in0=po[:], scalar=gn[:, 0:1], in1=rr,
                                                         op0=ALU.mult, op1=ALU.mult),
                 [po_t, gn_t, rr_t], [yt_t])
            P.op("dve", lambda e: e.tensor_tensor(out=YG[:, sl], in0=yt, in1=GS[:, sl], op=ALU.mult),
                 [yt_t, GS_t], [YG_t])

        for blk in range(NB):
            out_blk(blk)
        for h in range(2):
            P.dma("pool", out_d[hh][h], YG[:, h * TOK:(h + 1) * TOK], [YG_t], [B.yout_t])
        P.barrier()

    for head_i in range(2):
        one_head(head_i)


NKS = 12
TWO_PI = 6.283185307179586


def s5_branch(P, B, w_d, bblk_d, prm_d, c_d, dsk_d, out_d):
    ar = B.ar
    ar.reset()
    ps, ps_t = B.ps, B.ps_t
    tk = lambda n: P.tok(n)
    NF = 16
    ubf = ar.take(2 * SEQ * 2, BF16).rearrange("p (a t) -> p a t", a=2)
    Y32 = ar.take(2 * SEQ * 4, F32).rearrange("p (a t) -> p a t", a=2)
    bst = ar.take(16 * 128 * 4, F32)
    Bb = ar.take(16 * 128 * 2, BF16).rearrange("p (t d r g c) -> p t d r g c", t=2, d=2, r=2, g=2)
    prm = ar.take(NF * 3 * 4, F32).rearrange("p (f c) -> p f c", c=3)
    Cin = ar.take(NF * 2 * 16 * 4, F32).rearrange("p (f r h) -> p f r h", r=2, h=16)
    Cp = ar.take(NF * 2 * 16 * 4, F32).rearrange("p (f r h) -> p f r h", r=2, h=16)
    Cblk = ar.take(NF * 2 * 128 * 4, F32).rearrange("p (f r c) -> p f r c", r=2, c=128)
    LAM = ar.take(NKS * 3 * NF * 4, F32).rearrange("p (k c f) -> p k c f", k=NKS, c=3)
    sm = [ar.take(NF * 4, F32) for _ in range(12)]
    smi = ar.take(NF * 4, mybir.dt.int32)
    ctmp = [ar.take(NF * 16 * 4, F32).rearrange("p (f h) -> p f h", h=16) for _ in range(2)]
    dsk = ar.take(8, F32)
    ytmp = ar.take(512 * 4, F32)
    YS = ar.take(2 * SEQ * 2, BF16).rearrange("p (a t) -> p a t", a=2)
    (ubf_t, Y_t, bst_t, Bb_t, prm_t, Cin_t, Cp_t, Cblk_t, LAM_t, sm_t, dsk_t, ytmp_t, YS_t) = (
        tk("s5") for _ in range(13))
    load_weights(P, B, w_d, 256)
    P.dma("sp", bst, bblk_d.rearrange("p t d r g c -> p (t d r g c)"), [], [bst_t])
    P.dma("sp", prm, prm_d, [], [prm_t])
    P.dma("sp", Cin, c_d, [], [Cin_t])
    P.dma("sp", dsk, dsk_d, [], [dsk_t])
    P.op("pool", lambda e: e.tensor_copy(out=Bb.rearrange("p t d r g c -> p (t d r g c)"), in_=bst),
         [bst_t], [Bb_t])
    P.op("pool", lambda e: e.memset(Y32, 0.0), [], [Y_t])
    P.op("pool", lambda e: e.memset(Cblk, 0.0), [], [Cblk_t])

    def V(fn, reads=(), eng="dve"):
        P.op(eng, fn, [sm_t] + list(reads), [sm_t])

    lre, lim, ldt = prm[:, :, 0], prm[:, :, 1], prm[:, :, 2]
    dt, mag, ang, u_, nf, r_, m_, sn, cs_, t1, t2, inv = sm
    V(lambda e: e.activation(out=dt, in_=ldt, func=AF.Exp), [prm_t], "act")
    V(lambda e: e.tensor_tensor(out=t1, in0=lre, in1=dt, op=ALU.mult), [prm_t])
    V(lambda e: e.activation(out=mag, in_=t1, func=AF.Exp), [], "act")
    V(lambda e: e.tensor_tensor(out=ang, in0=lim, in1=dt, op=ALU.mult), [prm_t])

    def sin_of(dst, shift):
        V(lambda e: e.tensor_scalar(out=u_, in0=ang, scalar1=shift, scalar2=1.0 / TWO_PI,
                                    op0=ALU.add, op1=ALU.mult))
        V(lambda e: e.tensor_copy(out=smi, in_=u_))
        V(lambda e: e.tensor_copy(out=nf, in_=smi))
        V(lambda e: e.tensor_tensor(out=r_, in0=u_, in1=nf, op=ALU.subtract))
        V(lambda e: e.tensor_single_scalar(out=m_, in_=r_, scalar=0.5, op=ALU.is_gt))
        V(lambda e: e.tensor_tensor(out=r_, in0=r_, in1=m_, op=ALU.subtract))
        V(lambda e: e.tensor_single_scalar(out=m_, in_=r_, scalar=-0.5, op=ALU.is_lt))
        V(lambda e: e.tensor_tensor(out=r_, in0=r_, in1=m_, op=ALU.add))
        V(lambda e: e.activation(out=dst, in_=r_, func=AF.Sin, scale=TWO_PI), [], "act")

    sin_of(sn, 0.0)
    sin_of(cs_, TWO_PI / 4)
    L0re, L0im, L0nim = LAM[:, 0, 0, :], LAM[:, 0, 1, :], LAM[:, 0, 2, :]
    V(lambda e: e.tensor_tensor(out=L0re, in0=mag, in1=cs_, op=ALU.mult))
    V(lambda e: e.tensor_tensor(out=L0im, in0=mag, in1=sn, op=ALU.mult))
    V(lambda e: e.tensor_scalar(out=L0nim, in0=L0im, scalar1=-1.0, scalar2=None, op0=ALU.mult))

    def square_step(k):
        a_re, a_im = LAM[:, k - 1, 0, :], LAM[:, k - 1, 1, :]
        o_re, o_im, o_nim = LAM[:, k, 0, :], LAM[:, k, 1, :], LAM[:, k, 2, :]
        V(lambda e: e.tensor_tensor(out=t1, in0=a_re, in1=a_re, op=ALU.mult))
        V(lambda e: e.tensor_tensor(out=t2, in0=a_im, in1=a_im, op=ALU.mult))
        V(lambda e: e.tensor_tensor(out=o_re, in0=t1, in1=t2, op=ALU.subtract))
        V(lambda e: e.scalar_tensor_tensor(out=o_im, in0=a_re, scalar=2.0, in1=a_im,
                                           op0=ALU.mult, op1=ALU.mult))
        V(lambda e: e.tensor_scalar(out=o_nim, in0=o_im, scalar1=-1.0, scalar2=None, op0=ALU.mult))

    for k in range(1, NKS):
        square_step(k)
    V(lambda e: e.tensor_scalar(out=u_, in0=L0re, scalar1=-1.0, scalar2=None, op0=ALU.add))
    V(lambda e: e.tensor_tensor(out=t1, in0=lre, in1=lre, op=ALU.mult), [prm_t])
    V(lambda e: e.tensor_tensor(out=t2, in0=lim, in1=lim, op=ALU.mult), [prm_t])
    V(lambda e: e.tensor_tensor(out=t1, in0=t1, in1=t2, op=ALU.add))
    V(lambda e: e.reciprocal(out=inv, in_=t1))
    V(lambda e: e.tensor_tensor(out=t1, in0=u_, in1=lre, op=ALU.mult), [prm_t])
    V(lambda e: e.tensor_tensor(out=t2, in0=L0im, in1=lim, op=ALU.mult), [prm_t])
    V(lambda e: e.tensor_tensor(out=t1, in0=t1, in1=t2, op=ALU.add))
    V(lambda e: e.tensor_tensor(out=sn, in0=t1, in1=inv, op=ALU.mult))
    V(lambda e: e.tensor_tensor(out=t1, in0=L0im, in1=lre, op=ALU.mult), [prm_t])
    V(lambda e: e.tensor_tensor(out=t2, in0=u_, in1=lim, op=ALU.mult), [prm_t])
    V(lambda e: e.tensor_tensor(out=t1, in0=t1, in1=t2, op=ALU.subtract))
    V(lambda e: e.tensor_tensor(out=cs_, in0=t1, in1=inv, op=ALU.mult))
    crb = sn.unsqueeze(2).to_broadcast([128, NF, 16])
    cib = cs_.unsqueeze(2).to_broadcast([128, NF, 16])
    Cre, Cim = Cin[:, :, 0, :], Cin[:, :, 1, :]
    V(lambda e: e.tensor_tensor(out=ctmp[0], in0=Cre, in1=crb, op=ALU.mult), [Cin_t])
    V(lambda e: e.tensor_tensor(out=ctmp[1], in0=Cim, in1=cib, op=ALU.mult), [Cin_t])
    V(lambda e: e.tensor_tensor(out=Cp[:, :, 0, :], in0=ctmp[0], in1=ctmp[1], op=ALU.subtract))
    V(lambda e: e.tensor_tensor(out=ctmp[0], in0=Cre, in1=cib, op=ALU.mult), [Cin_t])
    V(lambda e: e.tensor_tensor(out=ctmp[1], in0=Cim, in1=crb, op=ALU.mult), [Cin_t])
    V(lambda e: e.tensor_tensor(out=ctmp[0], in0=ctmp[0], in1=ctmp[1], op=ALU.add))
    V(lambda e: e.tensor_scalar(out=Cp[:, :, 1, :], in0=ctmp[0], scalar1=-1.0, scalar2=None, op0=ALU.mult))

    def place(a, gp):
        j = gp % 4
        c0 = 32 * j + 16 * a
        rows = slice(64 * a, 64 * a + 64)
        P.op("dve", lambda e: e.tensor_copy(out=Cblk[rows, 2 * gp:2 * gp + 2, :, c0:c0 + 16],
                                            in_=Cp[rows, 2 * gp:2 * gp + 2, :, :]),
             [sm_t, Cblk_t], [Cblk_t])

    for a in range(2):
        for gp in range(8):
            place(a, gp)

    def u_blk(tile, blk):
        pu, pu_t = ps[(tile * NB + blk) % 2], ps_t[(tile * NB + blk) % 2]
        proj_feat(P, B, tile * 128, 128, blk, pu, pu_t)
        P.op("act", lambda e: e.copy(out=ubf[:, tile, blk * 512:(blk + 1) * 512], in_=pu[:]), [pu_t], [ubf_t])

    for tile in range(2):
        for blk in range(NB):
            u_blk(tile, blk)

    ks = [[B.hn[:, 2 * (2 * s_ + r) : 2 * (2 * s_ + r) + 2, :].bitcast(F32).rearrange("p a t -> p (a t)")
           for r in range(2)] for s_ in range(2)]
    Bd = [[P.tok(f"ksb{s_}{r}") for r in range(2)] for s_ in range(2)]
    Hd = [[P.tok(f"ksh{s_}{r}") for r in range(2)] for s_ in range(2)]

    def scan_pass(gp, d):
        tile, q, gpl = gp // 4, (gp % 4) // 2, gp % 2
        f = 2 * gp + d
        rows = slice(64 * q, 64 * q + 64)

        def bu(blk):
            sl = slice(blk * 512, (blk + 1) * 512)
            for r in range(2):
                pp, pp_t = ps[2 + r], ps_t[2 + r]
                P.op("pe", lambda e, r=r, pp=pp: e.matmul(pp[:], lhsT=Bb[rows, tile, d, r, gpl, :],
                                                         rhs=ubf[rows, tile, sl], start=True, stop=True),
                     [Bb_t, ubf_t], [pp_t])
                P.op("act", lambda e, r=r, pp=pp: e.copy(out=ks[0][r][:, sl], in_=pp[:]), [pp_t],
                     [Bd[0][r], Hd[0][r], B.hn_t])

        for blk in range(NB):
            bu(blk)

        def ks_round(k):
            s = 1 << k
            si, di = k % 2, (k + 1) % 2
            src, dst = ks[si], ks[di]
            if d == 0:
                hi, lo, hd = slice(s, SEQ), slice(0, SEQ - s), slice(0, s)
            else:
                hi, lo, hd = slice(0, SEQ - s), slice(s, SEQ), slice(SEQ - s, SEQ)
            a_re = LAM[:, k, 0, f:f + 1]
            a_im = LAM[:, k, 1, f:f + 1]
            a_nim = LAM[:, k, 2, f:f + 1]
            stt = lambda e, o, i0, sc, i1: e.scalar_tensor_tensor(out=o, in0=i0, scalar=sc, in1=i1,
                                                                  op0=ALU.mult, op1=ALU.add)
            S0 = [Bd[si][0], Hd[si][0]]
            S1 = [Bd[si][1], Hd[si][1]]
            P.op("act", lambda e: e.copy(out=dst[0][:, hd], in_=src[0][:, hd]), S0, [Hd[di][0]])
            P.op("act", lambda e: e.copy(out=dst[1][:, hd], in_=src[1][:, hd]), S1, [Hd[di][1]])
            P.op("dve", lambda e: stt(e, dst[0][:, hi], src[0][:, lo], a_re, src[0][:, hi]), S0 + [sm_t], [Bd[di][0]])
            P.op("dve", lambda e: stt(e, dst[1][:, hi], src[1][:, lo], a_re, src[1][:, hi]), S1 + [sm_t], [Bd[di][1]])
            P.op("dve", lambda e: stt(e, dst[0][:, hi], src[1][:, lo], a_nim, dst[0][:, hi]),
                 S1 + [sm_t, Bd[di][0]], [Bd[di][0]])
            P.op("dve", lambda e: stt(e, dst[1][:, hi], src[0][:, lo], a_im, dst[1][:, hi]),
                 S0 + [sm_t, Bd[di][1]], [Bd[di][1]])

        for k in range(NKS):
            ks_round(k)

        def yout(blk):
            sl = slice(blk * 512, (blk + 1) * 512)
            py, py_t = ps[4 + blk % 2], ps_t[4 + blk % 2]
            P.op("pe", lambda e: e.matmul(py[:], lhsT=Cblk[:, f, 0, :], rhs=ks[0][0][:, sl], start=True, stop=False),
                 [Cblk_t, Bd[0][0], Hd[0][0]], [py_t])
            P.op("pe", lambda e: e.matmul(py[:], lhsT=Cblk[:, f, 1, :], rhs=ks[0][1][:, sl], start=False, stop=True),
                 [Cblk_t, Bd[0][1], Hd[0][1]], [py_t])
            P.op("dve", lambda e: e.tensor_tensor(out=Y32[:, tile, sl], in0=Y32[:, tile, sl], in1=py[:], op=ALU.add),
                 [py_t, Y_t], [Y_t])

        for blk in range(NB):
            yout(blk)

    for gp in range(8):
        for d in range(2):
            scan_pass(gp, d)

    def fin(tile, blk):
        sl = slice(blk * 512, (blk + 1) * 512)
        P.op("dve", lambda e: e.scalar_tensor_tensor(out=ytmp, in0=ubf[:, tile, sl], scalar=dsk[:, tile:tile + 1],
                                                     in1=Y32[:, tile, sl], op0=ALU.mult, op1=ALU.add),
             [ubf_t, dsk_t, Y_t], [ytmp_t])
        P.op("act", lambda e: e.activation(out=YS[:, tile, sl], in_=ytmp, func=AF.Gelu_apprx_tanh),
             [ytmp_t], [YS_t])

    for tile in range(2):
        for blk in range(NB):
            fin(tile, blk)
    for tile in range(2):
        for h in range(2):
            P.dma("pool", out_d[tile][h], YS[:, tile, h * TOK:(h + 1) * TOK], [YS_t], [B.yout_t])


def build_B(do_attn=True, do_gla=True, do_s5=True):
    P = Prog()
    hn_d = P.dram("hnT", [KD, 128, SEQ], BF16, "ExternalInput")
    wa_d = P.dram("w_attn", [128, KD, 832], F32, "ExternalInput")
    cos_d = P.dram("rope_cos", [128, SEQ], F32, "ExternalInput")
    sin_d = P.dram("rope_sin", [128, SEQ], F32, "ExternalInput")
    gq_d = P.dram("gq", [128, 2], F32, "ExternalInput")
    gk_d = P.dram("gk", [128, 2], F32, "ExternalInput")
    wg_d = P.dram("w_gla", [128, KD, 1056], F32, "ExternalInput")
    wal_d = P.dram("wal", [32, 2, 256], F32, "ExternalInput")
    nb_d = P.dram("b_alpha", [128, 2, 2], F32, "ExternalInput")
    gn_d = P.dram("gla_gain", [128, 1], F32, "ExternalInput")
    cst_d = P.dram("cst", [128, 3, 128], F32, "ExternalInput")
    ws_d = P.dram("w_s5", [128, KD, 256], F32, "ExternalInput")
    bblk_d = P.dram("s5_bblk", [128, 2, 2, 2, 2, 128], F32, "ExternalInput")
    prm_d = P.dram("s5_prm", [128, 16, 3], F32, "ExternalInput")
    c_d = P.dram("s5_c", [128, 16, 2, 16], F32, "ExternalInput")
    dsk_d = P.dram("s5_d", [128, 2], F32, "ExternalInput")
    ya_d = P.dram("y_attn", [2, 128, SEQ], BF16, "ExternalOutput")
    yg_d = P.dram("y_gla", [2, 128, SEQ], BF16, "ExternalOutput")
    ys_d = P.dram("y_s5", [2, 128, SEQ], BF16, "ExternalOutput")
    B = setup_B(P)
    for k in range(KD):
        P.dma("sp", B.hn[:, k, :], hn_d[k], [], [B.hn_t])
    halves = lambda d: [[d[t][:, h * TOK:(h + 1) * TOK] for h in range(2)] for t in range(2)]
    ya_d, yg_d, ys_d = halves(ya_d), halves(yg_d), halves(ys_d)
    if do_attn:
        attn_branch(P, B, wa_d, cos_d, sin_d, gq_d, gk_d, ya_d)
        P.barrier()
    if do_gla:
        gla_branch(P, B, wg_d, wal_d, nb_d, gn_d, cst_d, yg_d)
        P.barrier()
    if do_s5:
        s5_branch(P, B, ws_d, bblk_d, prm_d, c_d, dsk_d, ys_d)
    P.barrier()
    return P.emit()


S5_W, GLA_W, ATT_W, ATT_KV = 512, 512, 512, 128
OFF_S5 = 0
OFF_GQ, OFF_GK, OFF_GV, OFF_GG = 512, 1024, 1536, 2048
OFF_ZF, OFF_ZB = 2560, 2576
OFF_AQ, OFF_AK, OFF_AV = 2592, 3104, 3232


def w_tile(w):
    return np.ascontiguousarray(w.reshape(KD, 128, w.shape[1]).transpose(1, 0, 2))


def rope_partner(n_heads):
    idx = np.arange(64)
    within = idx % 32
    partner = np.where(within < 16, idx + 16, idx - 16)
    return np.concatenate([h * 64 + partner for h in range(n_heads)])


def rope_tables():
    half = 16
    inv_freq = (10000.0 ** (-np.arange(half, dtype=np.float32) * 2.0 / 32)).astype(np.float32)
    t = np.arange(SEQ)
    rows = (t // 64).astype(np.float32)
    cols = (t % 64).astype(np.float32)
    ang = np.zeros((64, SEQ), np.float32)
    sign = np.zeros((64, 1), np.float32)
    for i in range(64):
        pos = rows if i < 32 else cols
        ang[i] = pos * inv_freq[i % 16]
        sign[i] = -1.0 if (i % 32) < 16 else 1.0
    cos = np.cos(ang).astype(np.float32)
    sin = (np.sin(ang) * sign).astype(np.float32)
    return np.concatenate([cos, cos], 0), np.concatenate([sin, sin], 0)


def gla_consts():
    j = np.arange(128)[:, None]
    i = np.arange(128)[None, :]
    c = np.zeros((128, 3, 128), np.float32)
    c[:, 0, :] = (j <= i)
    c[:, 1, :] = (j >= i)
    c[:, 2, :] = (j == i)
    return c


def maps_B(inp, li, hn_full):
    w_in = inp["w_in"][li]
    cosT, sinT = rope_tables()
    cst = gla_consts()
    maps = []
    for c in range(NCORES):
        b, k = c // 2, c % 2
        m = {"rope_cos": cosT, "rope_sin": sinT, "cst": cst}
        if hn_full[b] is not None:
            m["hnT"] = hn_full[b]
        q = w_in[:, OFF_AQ + 256 * k: OFF_AQ + 256 * (k + 1)]
        kk = w_in[:, OFF_AK + 64 * k: OFF_AK + 64 * (k + 1)]
        vv = w_in[:, OFF_AV + 64 * k: OFF_AV + 64 * (k + 1)]
        qrot = q[:, rope_partner(4)]
        krot = kk[:, rope_partner(1)]
        wa = np.concatenate([q, qrot, kk, kk, krot, krot, vv], axis=1)
        m["w_attn"] = w_tile(wa)
        gq = inp["attn_q_norm"][li]
        gk = inp["attn_k_norm"][li]
        pr = rope_partner(1)
        m["gq"] = np.ascontiguousarray(np.stack([np.tile(gq, 2), np.tile(gq[pr], 2)], 1))
        m["gk"] = np.ascontiguousarray(np.stack([np.tile(gk, 2), np.tile(gk[pr], 2)], 1))
        sl = slice(256 * k, 256 * (k + 1))
        wg = np.concatenate([w_in[:, OFF_GQ:OFF_GK][:, sl], w_in[:, OFF_GK:OFF_GV][:, sl],
                             w_in[:, OFF_GV:OFF_GG][:, sl], w_in[:, OFF_GG:OFF_ZF][:, sl],
                             w_in[:, OFF_ZF:OFF_AQ]], axis=1)
        m["w_gla"] = w_tile(wg)
        wal = np.zeros((32, 2, 256), np.float32)
        wal[0:16, 0, :] = inp["gla_w_alpha"][li, 0][:, sl]
        wal[16:32, 1, :] = inp["gla_w_alpha"][li, 1][:, sl]
        m["wal"] = wal
        ba = inp["gla_b_alpha"][li][:, sl].reshape(2, 2, 128)
        m["b_alpha"] = np.ascontiguousarray(ba.transpose(2, 0, 1))
        m["gla_gain"] = np.ascontiguousarray(inp["gla_norm"][li].reshape(128, 1))
        m["w_s5"] = w_tile(w_in[:, OFF_S5 + 256 * k: OFF_S5 + 256 * (k + 1)])
        g0 = 16 * k
        bb = np.zeros((128, 2, 2, 2, 2, 128), np.float32)
        braw = [inp["s5_b_re"][li], inp["s5_b_im"][li]]
        for tile in range(2):
            for p in range(128):
                g = g0 + tile * 8 + p // 16
                h = p % 16
                gpl = (p % 64) // 32
                a = (p % 32) // 16
                for d in range(2):
                    for r in range(2):
                        bb[p, tile, d, r, gpl, a * 64:(a + 1) * 64] = braw[r][d, g, :, h]
        m["s5_bblk"] = bb
        prm = np.zeros((128, 16, 3), np.float32)
        cc = np.zeros((128, 16, 2, 16), np.float32)
        craw = [inp["s5_c_re"][li], inp["s5_c_im"][li]]
        for gp in range(8):
            for a in range(2):
                g = g0 + 2 * gp + a
                for d in range(2):
                    f = 2 * gp + d
                    prm[a * 64:(a + 1) * 64, f, 0] = inp["s5_lambda_re"][li, d, g]
                    prm[a * 64:(a + 1) * 64, f, 1] = inp["s5_lambda_im"][li, d, g]
                    prm[a * 64:(a + 1) * 64, f, 2] = inp["s5_log_dt"][li, d, g]
                    for r in range(2):
                        cc[a * 64:(a + 1) * 64, f, r, :] = craw[r][d, g].T
        m["s5_prm"] = prm
        m["s5_c"] = cc
        m["s5_d"] = np.ascontiguousarray(inp["s5_d"][li][256 * k:256 * (k + 1)].reshape(2, 128).T)
        maps.append(m)
    return maps


def merge_phase(P, C, ys_d, yg_d, ya_d, wglu_d, wbr_d, wgate_d, bgate_d, wout_d, y_all=None, sel_d=None):
    half = C.ntok // 2
    bg, bg_t = load_vec(P, C, "bgate_sb", bgate_d, 24)
    if y_all is not None:
        sel, sel_t = load_vec(P, C, "sel_sb", sel_d, 2)
        ytmp = [P.sbuf(f"ytmp{i}", [128, half], BF16) for i in range(2)]
        ytmp_t = [P.tok() for i in range(2)]
    for tt in range(2):
        P.barrier()
        A = C.arena
        o = 0

        def take(nwords, shape_c):
            nonlocal o
            ap = A[:, o:o + nwords].bitcast(BF16).rearrange("p (c t) -> p c t", c=shape_c)
            o += nwords
            return ap

        yb = [take(4 * half // 2, 4) for _ in range(3)]
        ysg = take(4 * half // 2, 4)
        mg = take(8 * half // 2, 8)
        yb_t = [P.tok() for _ in range(3)]
        ysg_t, mg_t = P.tok(), P.tok()
        t0 = tt * half
        def pick(br, gt):
            for j in range(2):
                r_ = gt // 2
                P.dma("sp", ytmp[j], y_all[br][gt % 2][j][r_ * 128:(r_ + 1) * 128, t0:t0 + half],
                      [C.yall_t], [ytmp_t[j]])
            P.op("dve", lambda e: e.tensor_scalar(out=ytmp[0], in0=ytmp[0], scalar1=sel[:, 0:1], scalar2=None,
                                                  op0=ALU.mult), [ytmp_t[0], sel_t], [ytmp_t[0]])
            P.op("dve", lambda e: e.scalar_tensor_tensor(out=yb[br][:, gt, :], in0=ytmp[1], scalar=sel[:, 1:2],
                                                         in1=ytmp[0], op0=ALU.mult, op1=ALU.add),
                 [ytmp_t[0], ytmp_t[1], sel_t], [yb_t[br]])

        for br, src in enumerate((ys_d, yg_d, ya_d)):
            for k in range(4):
                if y_all is None:
                    P.dma("sp", yb[br][:, k, :], src[k, :, t0:t0 + half], [], [yb_t[br]])
                else:
                    pick(br, k)
        nb2 = half // 512

        def wload(src_ap, nchunk, slot_cols):
            i = C.wcnt % 2
            C.wcnt += 1
            st = C.wst[i].rearrange("p a k f -> p (a k) f")
            wb = C.wb[i].rearrange("p a k f -> p (a k) f")
            P.dma("sp", st[:, slot_cols:slot_cols + nchunk, :], src_ap, [], [C.wst_t[i]])
            P.op("pool", lambda e: e.tensor_copy(out=wb[:, slot_cols:slot_cols + nchunk, :],
                                                 in_=st[:, slot_cols:slot_cols + nchunk, :]),
                 [C.wst_t[i]], [C.wb_t[i]])
            return wb, C.wb_t[i]

        def glu_m(m):
            wb, wb_t = wload(wglu_d[m], 4, 0)
            for b in range(nb2):
                sl = slice(b * 512, (b + 1) * 512)
                j = C.gcnt % 2
                C.gcnt += 1
                for k in range(4):
                    P.op("pe", lambda e, k=k, j=j, sl=sl: e.matmul(C.pg[j][:], lhsT=wb[:, k, :], rhs=yb[0][:, k, sl],
                                                                 start=(k == 0), stop=(k == 3)),
                         [wb_t, yb_t[0]], [C.pg_t[j]])
                P.op("act", lambda e, j=j: e.activation(out=C.sg[j][:], in_=C.pg[j][:], func=AF.Sigmoid),
                     [C.pg_t[j]], [C.sg_t[j]])
                P.op("dve", lambda e, j=j, sl=sl: e.tensor_tensor(out=ysg[:, m, sl], in0=C.sg[j][:],
                                                                 in1=yb[0][:, m, sl], op=ALU.mult),
                     [C.sg_t[j], yb_t[0]], [ysg_t])

        for m in range(4):
            glu_m(m)

        def merge_m(m):
            for br in range(3):
                ysrc, ysrc_t = (ysg, ysg_t) if br == 0 else (yb[br], yb_t[br])
                i = C.wcnt % 2
                wb, wb_t = wload(wbr_d[br, m], 4, 0)
                st = C.wst[i].rearrange("p a k f -> p (a k) f")
                P.dma("sp", st[:, 4:12, :], wgate_d[br * 8 + m], [], [C.wst_t[i]])
                P.op("pool", lambda e, st=st, wb=wb: e.tensor_copy(out=wb[:, 4:12, :], in_=st[:, 4:12, :]),
                     [C.wst_t[i]], [wb_t])
                for b in range(nb2):
                    sl = slice(b * 512, (b + 1) * 512)
                    tsl = slice(t0 + b * 512, t0 + (b + 1) * 512)
                    j = C.gcnt % 2
                    C.gcnt += 1
                    for k in range(4):
                        P.op("pe", lambda e, k=k, j=j, sl=sl, wb=wb, ysrc=ysrc: e.matmul(
                            C.pg[j][:], lhsT=wb[:, k, :], rhs=ysrc[:, k, sl], start=(k == 0), stop=(k == 3)),
                            [wb_t, ysrc_t], [C.pg_t[j]])
                    for k in range(KD):
                        P.op("pe", lambda e, k=k, j=j, tsl=tsl, wb=wb: e.matmul(
                            C.pu[j][:], lhsT=wb[:, 4 + k, :], rhs=C.hn[:, k, tsl], start=(k == 0), stop=(k == KD - 1)),
                            [wb_t] + C.hnt, [C.pu_t[j]])
                    P.op("act", lambda e, j=j, br=br: e.activation(
                        out=C.sg[j][:], in_=C.pu[j][:], func=AF.Sigmoid, bias=bg[:, br * 8 + m:br * 8 + m + 1]),
                        [C.pu_t[j], bg_t], [C.sg_t[j]])
                    if br == 0:
                        P.op("dve", lambda e, j=j, b=b: e.tensor_tensor(
                            out=C.macc[b][:], in0=C.sg[j][:], in1=C.pg[j][:], op=ALU.mult),
                            [C.sg_t[j], C.pg_t[j]], [C.macc_t[b]])
                    else:
                        P.op("dve", lambda e, j=j: e.tensor_tensor(
                            out=C.sg[j][:], in0=C.sg[j][:], in1=C.pg[j][:], op=ALU.mult),
                            [C.sg_t[j], C.pg_t[j]], [C.sg_t[j]])
                        if br == 1:
                            P.op("dve", lambda e, j=j, b=b: e.tensor_tensor(
                                out=C.macc[b][:], in0=C.macc[b][:], in1=C.sg[j][:], op=ALU.add),
                                [C.sg_t[j], C.macc_t[b]], [C.macc_t[b]])
                        else:
                            P.op("dve", lambda e, j=j, b=b, sl=sl: e.tensor_tensor(
                                out=mg[:, m, sl], in0=C.macc[b][:], in1=C.sg[j][:], op=ALU.add),
                                [C.sg_t[j], C.macc_t[b]], [mg_t])

        for m in range(KD):
            merge_m(m)

        def out_m(m):
            wb, wb_t = wload(wout_d[m], 8, 0)
            for b in range(nb2):
                sl = slice(b * 512, (b + 1) * 512)
                tsl = slice(t0 + b * 512, t0 + (b + 1) * 512)
                gb = (t0 + b * 512) // 512
                j = C.ocnt % 2
                C.ocnt += 1
                for k in range(KD):
                    P.op("pe", lambda e, k=k, j=j, sl=sl: e.matmul(C.po[j][:], lhsT=wb[:, k, :], rhs=mg[:, k, sl],
                                                                 start=(k == 0), stop=(k == KD - 1)),
                         [wb_t, mg_t], [C.po_t[j]])
                P.op("dve", lambda e, j=j, tsl=tsl: e.tensor_tensor(out=C.x[:, m, tsl], in0=C.x[:, m, tsl],
                                                                    in1=C.po[j][:], op=ALU.add),
                     [C.po_t[j], C.xt[gb]], [C.xt[gb]])

        for m in range(KD):
            out_m(m)
    P.barrier()


def build_C(last):
    P = Prog()
    x_d = P.dram("xT", [KD, 128, TOK], F32, "ExternalInput")
    hn_d = P.dram("hnT", [KD, 128, TOK], BF16, "ExternalInput")
    ys_d = P.dram("ys5", [4, 128, TOK], BF16, "ExternalInput")
    yg_d = P.dram("ygla", [4, 128, TOK], BF16, "ExternalInput")
    ya_d = P.dram("yattn", [4, 128, TOK], BF16, "ExternalInput")
    wglu_d = P.dram("wglu", [4, 128, 4, 128], F32, "ExternalInput")
    wbr_d = P.dram("wbr", [3, KD, 128, 4, 128], F32, "ExternalInput")
    wgate_d = P.dram("wgate", [24, 128, KD, 128], F32, "ExternalInput")
    bgate_d = P.dram("bgate", [128, 24], F32, "ExternalInput")
    wout_d = P.dram("wout", [KD, 128, KD, 128], F32, "ExternalInput")
    g2_d = P.dram("g_ffn2", [128, KD], F32, "ExternalInput")
    wg2_d = P.dram("wg2", [NFF, 128, KD, 128], F32, "ExternalInput")
    wu2_d = P.dram("wu2", [NFF, 128, KD, 128], F32, "ExternalInput")
    wd2_d = P.dram("wd2", [KD, 128, NFF, 128], F32, "ExternalInput")
    g3_d = P.dram("g_next", [128, KD], F32, "ExternalInput")
    if not last:
        wg1_d = P.dram("wg1", [NFF, 128, KD, 128], F32, "ExternalInput")
        wu1_d = P.dram("wu1", [NFF, 128, KD, 128], F32, "ExternalInput")
        wd1_d = P.dram("wd1", [KD, 128, NFF, 128], F32, "ExternalInput")
        g4_d = P.dram("g_mix", [128, KD], F32, "ExternalInput")
        xo_d = P.dram("xo", [KD, 128, TOK], F32, "ExternalOutput")
        hno_d = P.dram("hno", [KD, 128, TOK], BF16, "ExternalOutput")
    else:
        out_d = P.dram("outT", [KD, 128, TOK], F32, "ExternalOutput")
    C = setup_common(P)
    eps_const(P, C)
    C.macc = [P.sbuf(f"macc{i}", [128, 512], F32) for i in range(2)]
    C.macc_t = [P.tok() for i in range(2)]
    g2, g2t = load_vec(P, C, "g2", g2_d, KD)
    g3, g3t = load_vec(P, C, "g3", g3_d, KD)
    load_x(P, C, x_d)
    for k in range(KD):
        for b in range(C.nblk):
            sl = slice(b * 512, (b + 1) * 512)
            P.dma("sp", C.hn[:, k, sl], hn_d[k, :, sl], [], [C.hnt[b]])
    ffn_alloc(P, C)
    merge_phase(P, C, ys_d, yg_d, ya_d, wglu_d, wbr_d, wgate_d, bgate_d, wout_d)
    for b in range(C.nblk):
        rmsnorm_block(P, C, b, g2, g2t, C.hn, C.hnt)
    ffn(P, C, wg2_d, wu2_d, wd2_d)
    if not last:
        g4, g4t = load_vec(P, C, "g4", g4_d, KD)
        for b in range(C.nblk):
            rmsnorm_block(P, C, b, g3, g3t, C.hn, C.hnt)
        ffn(P, C, wg1_d, wu1_d, wd1_d)
        for b in range(C.nblk):
            rmsnorm_block(P, C, b, g4, g4t, C.hn, C.hnt)
        store_feat(P, C, C.x, C.xt, xo_d)
        store_feat(P, C, C.hn, C.hnt, hno_d)
    else:
        for b in range(C.nblk):
            rmsnorm_block(P, C, b, g3, g3t, C.x, C.xt)
        store_feat(P, C, C.x, C.xt, out_d)
    P.barrier()
    return P.emit()


def _run(nc, maps):
    res = run_bass_kernel_spmd(nc, maps, core_ids=list(range(NCORES)))
    return res.results


def _ffn_maps(inp, pre, li, names):
    return {names[0]: tile_w_kc(inp[pre + "_w_gate"][li]), names[1]: tile_w_kc(inp[pre + "_w_up"][li]),
            names[2]: tile_w_kc(inp[pre + "_w_down"][li])}


def kernel_unfused(**inputs):
    inp = {k: np.asarray(v) for k, v in inputs.items()}
    x = inp["x"].astype(np.float32)
    base = _ffn_maps(inp, "ffn1", 0, ("wg", "wu", "wd"))
    base["g_ffn"] = vec128(inp["ffn1_norm"][0])
    base["g_mix"] = vec128(inp["mix_norm"][0])
    maps = []
    for c in range(NCORES):
        b, k = c // 2, c % 2
        m = dict(base)
        m["xT"] = feat_major(x[b, k * TOK:(k + 1) * TOK])
        maps.append(m)
    res = _run(build_A(), maps)
    xT = [r["xo"] for r in res]
    hnT = [r["hno"] for r in res]
    ncB = build_B()
    out = np.zeros((BATCH, SEQ, D_MODEL), np.float32)
    for li in range(DEPTH):
        last = li == DEPTH - 1
        hn_full = [np.ascontiguousarray(np.concatenate([hnT[2 * b], hnT[2 * b + 1]], axis=2)) for b in range(BATCH)]
        resB = _run(ncB if li == 0 else build_B(), maps_B(inp, li, hn_full))
        base = _ffn_maps(inp, "ffn2", li, ("wg2", "wu2", "wd2"))
        base["g_ffn2"] = vec128(inp["ffn2_norm"][li])
        base["wglu"] = tile_w_kc(inp["s5_w_glu"][li])
        base["wbr"] = np.stack([tile_w_kc(inp["w_branch_s5"][li]), tile_w_kc(inp["w_branch_gla"][li]),
                                tile_w_kc(inp["w_branch_attn"][li])], 0)
        base["wgate"] = tile_w_kc(inp["w_merge_gate"][li])
        base["bgate"] = vec128(inp["b_merge_gate"][li])
        base["wout"] = tile_w_kc(inp["w_out"][li])
        if not last:
            base.update(_ffn_maps(inp, "ffn1", li + 1, ("wg1", "wu1", "wd1")))
            base["g_next"] = vec128(inp["ffn1_norm"][li + 1])
            base["g_mix"] = vec128(inp["mix_norm"][li + 1])
        else:
            base["g_next"] = vec128(inp["final_norm"])
        maps = []
        for c in range(NCORES):
            b, k = c // 2, c % 2
            ts = slice(k * TOK, (k + 1) * TOK)
            m = dict(base)
            m["xT"] = xT[c]
            m["hnT"] = hnT[c]
            for key, okey in (("ys5", "y_s5"), ("ygla", "y_gla"), ("yattn", "y_attn")):
                full = np.concatenate([resB[2 * b][okey], resB[2 * b + 1][okey]], axis=0)
                m[key] = np.ascontiguousarray(full[:, :, ts])
            maps.append(m)
        res = _run(build_C(last), maps)
        if not last:
            xT = [r["xo"] for r in res]
            hnT = [r["hno"] for r in res]
        else:
            for c in range(NCORES):
                b, k = c // 2, c % 2
                out[b, k * TOK:(k + 1) * TOK, :] = res[c]["outT"].reshape(D_MODEL, TOK).T
    return out


MEGA_WORDS = 53000


def _ffn_decl(P, pre):
    return (P.dram(pre + "_wg", [NFF, 128, KD, 128], F32, "ExternalInput"),
            P.dram(pre + "_wu", [NFF, 128, KD, 128], F32, "ExternalInput"),
            P.dram(pre + "_wd", [KD, 128, NFF, 128], F32, "ExternalInput"))


def build_fused():
    P = Prog()
    P.enable_mega(MEGA_WORDS)
    x_d = P.dram("xT", [KD, 128, TOK], F32, "ExternalInput")
    sel_d = P.dram("sel", [128, 2], F32, "ExternalInput")
    cos_d = P.dram("rope_cos", [128, SEQ], F32, "ExternalInput")
    sin_d = P.dram("rope_sin", [128, SEQ], F32, "ExternalInput")
    cst_d = P.dram("cst", [128, 3, 128], F32, "ExternalInput")
    gfin_d = P.dram("g_final", [128, KD], F32, "ExternalInput")
    L = []
    for li in range(DEPTH):
        d = Ctx()
        p = f"l{li}_"
        d.ffn1 = _ffn_decl(P, p + "ffn1")
        d.ffn2 = _ffn_decl(P, p + "ffn2")
        d.g_ffn1 = P.dram(p + "g_ffn1", [128, KD], F32, "ExternalInput")
        d.g_mix = P.dram(p + "g_mix", [128, KD], F32, "ExternalInput")
        d.g_ffn2 = P.dram(p + "g_ffn2", [128, KD], F32, "ExternalInput")
        d.wa = P.dram(p + "w_attn", [128, KD, 832], F32, "ExternalInput")
        d.gq = P.dram(p + "gq", [128, 2], F32, "ExternalInput")
        d.gk = P.dram(p + "gk", [128, 2], F32, "ExternalInput")
        d.wg = P.dram(p + "w_gla", [128, KD, 1056], F32, "ExternalInput")
        d.wal = P.dram(p + "wal", [32, 2, 256], F32, "ExternalInput")
        d.nb = P.dram(p + "b_alpha", [128, 2, 2], F32, "ExternalInput")
        d.gn = P.dram(p + "gla_gain", [128, 1], F32, "ExternalInput")
        d.ws = P.dram(p + "w_s5", [128, KD, 256], F32, "ExternalInput")
        d.bblk = P.dram(p + "s5_bblk", [128, 2, 2, 2, 2, 128], F32, "ExternalInput")
        d.prm = P.dram(p + "s5_prm", [128, 16, 3], F32, "ExternalInput")
        d.c = P.dram(p + "s5_c", [128, 16, 2, 16], F32, "ExternalInput")
        d.dsk = P.dram(p + "s5_d", [128, 2], F32, "ExternalInput")
        d.wglu = P.dram(p + "wglu", [4, 128, 4, 128], F32, "ExternalInput")
        d.wbr = P.dram(p + "wbr", [3, KD, 128, 4, 128], F32, "ExternalInput")
        d.wgate = P.dram(p + "wgate", [24, 128, KD, 128], F32, "ExternalInput")
        d.bgate = P.dram(p + "bgate", [128, 24], F32, "ExternalInput")
        d.wout = P.dram(p + "wout", [KD, 128, KD, 128], F32, "ExternalInput")
        L.append(d)
    out_d = P.dram("outT", [KD, 128, TOK], F32, "ExternalOutput")
    for li in range(DEPTH):
        d = L[li]
        p = f"l{li}_"
        d.hn_src = [P.dram(p + f"hn_src{k}", [128, TOK], BF16, "Internal") for k in range(KD)]
        d.hn_all = [P.dram(p + f"hn_all{k}", [256, TOK], BF16, "Internal") for k in range(KD)]
        d.y_src = [[[P.dram(p + f"y_src{b}{t}{h}", [128, TOK], BF16, "Internal") for h in range(2)]
                    for t in range(2)] for b in range(3)]
        d.y_all = [[[P.dram(p + f"y_all{b}{t}{h}", [256, TOK], BF16, "Internal") for h in range(2)]
                    for t in range(2)] for b in range(3)]
        d.x_sp = P.dram(p + "x_sp", [KD, 128, TOK], F32, "Internal")

    def park(C, d):
        xs_t = store_feat(P, C, C.x, C.xt, d.x_sp)
        all_t = P.tok("hn_all")
        for k in range(KD):
            t = P.tok()
            P.dma("pool", d.hn_src[k], C.hn[:, k, :], list(C.hnt), [t])
            P.collective(d.hn_src[k], d.hn_all[k], [t], [all_t])
        return xs_t, all_t

    C = setup_common(P)
    eps_const(P, C)
    g1, g1t = load_vec(P, C, "g1", L[0].g_ffn1, KD)
    g2, g2t = load_vec(P, C, "g2", L[0].g_mix, KD)
    load_x(P, C, x_d)
    for b in range(C.nblk):
        rmsnorm_block(P, C, b, g1, g1t, C.hn, C.hnt)
    ffn(P, C, *L[0].ffn1)
    for b in range(C.nblk):
        rmsnorm_block(P, C, b, g2, g2t, C.hn, C.hnt)
    xs_t, hn_all_t = park(C, L[0])

    for li in range(DEPTH):
        d = L[li]
        last = li == DEPTH - 1
        P.barrier()
        P.phase_reset()
        B = setup_B(P)
        for r in range(2):
            for k in range(KD):
                lt = P.tok()
                P.dma("sp", B.hn[:, k, r * TOK:(r + 1) * TOK], d.hn_all[k][r * 128:(r + 1) * 128, :],
                      [hn_all_t], [lt])
                B.hn_toks.append(lt)
        yv = d.y_src
        attn_branch(P, B, d.wa, cos_d, sin_d, d.gq, d.gk, yv[2])
        P.barrier()
        gla_branch(P, B, d.wg, d.wal, d.nb, d.gn, cst_d, yv[1])
        P.barrier()
        s5_branch(P, B, d.ws, d.bblk, d.prm, d.c, d.dsk, yv[0])
        P.barrier()
        yall_t = P.tok("y_all")
        for b_ in range(3):
            for t_ in range(2):
                for h_ in range(2):
                    P.collective(d.y_src[b_][t_][h_], d.y_all[b_][t_][h_], [], [yall_t])
        P.barrier()
        P.phase_reset()
        C = setup_common(P)
        C.yall_t = yall_t
        eps_const(P, C)
        C.macc = [P.sbuf(f"macc{i}", [128, 512], F32) for i in range(2)]
        C.macc_t = [P.tok() for i in range(2)]
        gA, gAt = load_vec(P, C, "gA", d.g_ffn2, KD)
        for k in range(KD):
            for b in range(C.nblk):
                sl = slice(b * 512, (b + 1) * 512)
                P.dma("sp", C.x[:, k, sl], d.x_sp[k, :, sl], list(xs_t), [C.xt[b]])
                P.dma("sp", C.hn[:, k, sl], d.hn_src[k][:, sl], [], [C.hnt[b]])
        ffn_alloc(P, C)
        merge_phase(P, C, None, None, None, d.wglu, d.wbr, d.wgate, d.bgate, d.wout,
                    y_all=d.y_all, sel_d=sel_d)
        for b in range(C.nblk):
            rmsnorm_block(P, C, b, gA, gAt, C.hn, C.hnt)
        ffn(P, C, *d.ffn2)
        if not last:
            n = L[li + 1]
            gB, gBt = load_vec(P, C, "gB", n.g_ffn1, KD)
            gC, gCt = load_vec(P, C, "gC", n.g_mix, KD)
            for b in range(C.nblk):
                rmsnorm_block(P, C, b, gB, gBt, C.hn, C.hnt)
            ffn(P, C, *n.ffn1)
            for b in range(C.nblk):
                rmsnorm_block(P, C, b, gC, gCt, C.hn, C.hnt)
            xs_t, hn_all_t = park(C, n)
        else:
            gF, gFt = load_vec(P, C, "gF", gfin_d, KD)
            for b in range(C.nblk):
                rmsnorm_block(P, C, b, gF, gFt, C.x, C.xt)
            store_feat(P, C, C.x, C.xt, out_d)
    P.barrier()
    return P.emit()


def maps_fused(inp):
    x = inp["x"].astype(np.float32)
    cosT, sinT = rope_tables()
    common = {"rope_cos": cosT, "rope_sin": sinT, "cst": gla_consts(), "g_final": vec128(inp["final_norm"])}
    per_layer_B = []
    dummy_hn = [None] * BATCH
    for li in range(DEPTH):
        p = f"l{li}_"
        for pre in ("ffn1", "ffn2"):
            common[p + pre + "_wg"] = tile_w_kc(inp[pre + "_w_gate"][li])
            common[p + pre + "_wu"] = tile_w_kc(inp[pre + "_w_up"][li])
            common[p + pre + "_wd"] = tile_w_kc(inp[pre + "_w_down"][li])
        common[p + "g_ffn1"] = vec128(inp["ffn1_norm"][li])
        common[p + "g_mix"] = vec128(inp["mix_norm"][li])
        common[p + "g_ffn2"] = vec128(inp["ffn2_norm"][li])
        common[p + "wglu"] = tile_w_kc(inp["s5_w_glu"][li])
        common[p + "wbr"] = np.stack([tile_w_kc(inp["w_branch_s5"][li]), tile_w_kc(inp["w_branch_gla"][li]),
                                      tile_w_kc(inp["w_branch_attn"][li])], 0)
        common[p + "wgate"] = tile_w_kc(inp["w_merge_gate"][li])
        common[p + "bgate"] = vec128(inp["b_merge_gate"][li])
        common[p + "wout"] = tile_w_kc(inp["w_out"][li])
        per_layer_B.append(maps_B(inp, li, dummy_hn))
    maps = []
    for c in range(NCORES):
        b, k = c // 2, c % 2
        m = dict(common)
        m["xT"] = feat_major(x[b, k * TOK:(k + 1) * TOK])
        sel = np.zeros((128, 2), np.float32)
        sel[:, k] = 1.0
        m["sel"] = sel
        for li in range(DEPTH):
            mb = per_layer_B[li][c]
            for key in ("w_attn", "gq", "gk", "w_gla", "wal", "b_alpha", "gla_gain", "w_s5", "s5_bblk", "s5_prm",
                        "s5_c", "s5_d"):
                m[f"l{li}_" + key] = mb[key]
        maps.append(m)
    return maps


def kernel_fused(**inputs):
    inp = {k: np.asarray(v) for k, v in inputs.items()}
    res = _run(build_fused(), maps_fused(inp))
    out = np.zeros((BATCH, SEQ, D_MODEL), np.float32)
    for c in range(NCORES):
        b, k = c // 2, c % 2
        out[b, k * TOK:(k + 1) * TOK, :] = res[c]["outT"].reshape(D_MODEL, TOK).T
    return out


def kernel(**inputs):
    return kernel_fused(**inputs)
```
